# Optimizing a Trainium2 kernel written in Bass

```python
import jax, jax.numpy as jnp
from jax import lax
import numpy as np

D_MODEL = 2048
BATCH = 2
SEQ = 16384
DEPTH = 1

CHUNK = 64
QUERY_BLOCK = 128
SB_HEADS = 8
SB_HEAD_DIM = 128
MLA_HEADS = 8
MLA_NOPE_DIM = 128
MLA_ROPE_DIM = 64
MLA_V_DIM = 128
MLA_Q_RANK = 768
MLA_KV_RANK = 512
N_BRANCHES = 2
D_FF = -(-8 * D_MODEL // (3 * 256)) * 256
ROPE_THETA = 10000.0
NORM_EPS = 1e-6
NEG_INF = -1e30

SB_WIDTH = SB_HEADS * SB_HEAD_DIM
MLA_QK_DIM = MLA_NOPE_DIM + MLA_ROPE_DIM
IN_WIDTHS = [SB_WIDTH, SB_WIDTH, SB_WIDTH, MLA_Q_RANK, MLA_KV_RANK, MLA_ROPE_DIM, N_BRANCHES * D_MODEL]
D_IN = int(sum(IN_WIDTHS))
IN_OFFSETS = [int(o) for o in np.cumsum(IN_WIDTHS)[:-1]]

kernel_name = "sandwich_gated_stickbreaking_mla_swiglu"


def rms_norm(x, g):
    xf = x.astype(jnp.float32)
    y = xf * lax.rsqrt(jnp.mean(xf * xf, axis=-1, keepdims=True) + NORM_EPS)
    return (y * g.astype(jnp.float32)).astype(x.dtype)


def rope(x, positions):
    half = MLA_ROPE_DIM // 2
    inv_freq = ROPE_THETA ** (-jnp.arange(half, dtype=jnp.float32) / half)
    ang = positions.astype(jnp.float32)[..., None] * inv_freq
    if x.ndim == 4:
        ang = ang[:, :, None, :]
    cos, sin = jnp.cos(ang), jnp.sin(ang)
    xf = x.astype(jnp.float32)
    x1, x2 = xf[..., :half], xf[..., half:]
    return jnp.concatenate([x1 * cos - x2 * sin, x2 * cos + x1 * sin], axis=-1).astype(x.dtype)


def to_blocks(t):
    b, s = t.shape[:2]
    return jnp.moveaxis(t.reshape(b, s // QUERY_BLOCK, QUERY_BLOCK, *t.shape[2:]), 1, 0)


def from_blocks(t):
    t = jnp.moveaxis(t, 0, 1)
    return t.reshape(t.shape[0], -1, *t.shape[3:])


def stick_breaking_attention(q, k, v):
    seq = k.shape[1]
    scale = SB_HEAD_DIM ** -0.5
    key_pos = jnp.arange(seq)

    def one_block(args):
        qb, blk = args
        z = jnp.einsum('bqhd,bkhd->bhqk', qb, k).astype(jnp.float32) * scale
        q_pos = blk * QUERY_BLOCK + jnp.arange(QUERY_BLOCK)
        valid = key_pos[None, :] < q_pos[:, None]
        log_keep = jnp.where(valid, jax.nn.log_sigmoid(-z), 0.0)
        log_suffix = lax.cumsum(log_keep, axis=3, reverse=True) - log_keep
        w = jnp.where(valid, jnp.exp(jax.nn.log_sigmoid(z) + log_suffix), 0.0)
        return jnp.einsum('bhqk,bkhd->bqhd', w.astype(v.dtype), v)

    n_blocks = q.shape[1] // QUERY_BLOCK
    return from_blocks(lax.map(one_block, (to_blocks(q), jnp.arange(n_blocks))))


def mla_attention(q_nope, q_rope, k_nope, k_rope, v):
    seq = k_nope.shape[1]
    scale = MLA_QK_DIM ** -0.5
    key_chunk = jnp.arange(seq) // CHUNK

    def one_block(args):
        qn, qr, blk = args
        s = (jnp.einsum('bqhd,bkhd->bhqk', qn, k_nope)
             + jnp.einsum('bqhr,bkr->bhqk', qr, k_rope)).astype(jnp.float32) * scale
        q_chunk = (blk * QUERY_BLOCK + jnp.arange(QUERY_BLOCK)) // CHUNK
        allowed = key_chunk[None, :] <= q_chunk[:, None]
        p = jax.nn.softmax(jnp.where(allowed, s, NEG_INF), axis=-1)
        return jnp.einsum('bhqk,bkhd->bqhd', p.astype(v.dtype), v)

    n_blocks = q_nope.shape[1] // QUERY_BLOCK
    out = lax.map(one_block, (to_blocks(q_nope), to_blocks(q_rope), jnp.arange(n_blocks)))
    return from_blocks(out)


def setup_inputs(seed: int = 0) -> dict:
    key = jax.random.key(seed)
    ks = jax.random.split(key, 20)
    f32 = jnp.float32

    def w(k, shape, fan_in):
        return jax.random.normal(k, shape, f32) * fan_in ** -0.5

    def gain(k, dim):
        return 1.0 + 0.1 * jax.random.normal(k, (DEPTH, dim), f32)

    x = jax.random.normal(ks[0], (BATCH, SEQ, D_MODEL), f32)
    offsets = jax.random.randint(ks[1], (BATCH, 1), 0, 4096) * CHUNK
    positions = (offsets + jnp.arange(SEQ, dtype=jnp.int32)[None, :]).astype(jnp.int32)
    return {
        "x": x,
        "positions": positions,
        "norm_pre_mix": gain(ks[2], D_MODEL),
        "w_in": w(ks[3], (DEPTH, D_MODEL, D_IN), D_MODEL),
        "b_gate": 0.1 * jax.random.normal(ks[4], (DEPTH, N_BRANCHES * D_MODEL), f32),
        "q_norm": gain(ks[5], MLA_Q_RANK),
        "w_uq": w(ks[6], (DEPTH, MLA_Q_RANK, MLA_HEADS * MLA_QK_DIM), MLA_Q_RANK),
        "kv_norm": gain(ks[7], MLA_KV_RANK),
        "w_ukv": w(ks[8], (DEPTH, MLA_KV_RANK, MLA_HEADS * (MLA_NOPE_DIM + MLA_V_DIM)), MLA_KV_RANK),
        "w_o_sb": w(ks[9], (DEPTH, SB_WIDTH, D_MODEL), SB_WIDTH),
        "w_o_mla": w(ks[10], (DEPTH, MLA_HEADS * MLA_V_DIM, D_MODEL), MLA_HEADS * MLA_V_DIM),
        "w_out": w(ks[11], (DEPTH, D_MODEL, D_MODEL), D_MODEL),
        "norm_post_mix": gain(ks[12], D_MODEL),
        "norm_pre_ffn": gain(ks[13], D_MODEL),
        "w_gate_up": w(ks[14], (DEPTH, D_MODEL, 2 * D_FF), D_MODEL),
        "w_down": w(ks[15], (DEPTH, D_FF, D_MODEL), D_FF),
        "norm_post_ffn": gain(ks[16], D_MODEL),
    }


def reference(x, positions, norm_pre_mix, w_in, b_gate, q_norm, w_uq, kv_norm, w_ukv,
              w_o_sb, w_o_mla, w_out, norm_post_mix, norm_pre_ffn, w_gate_up, w_down,
              norm_post_ffn):
    b, s, _ = x.shape
    h = x
    for l in range(DEPTH):
        u = rms_norm(h, norm_pre_mix[l])
        proj = u @ w_in[l]
        q_sb, k_sb, v_sb, q_lat, kv_lat, k_r, gate_logits = jnp.split(proj, IN_OFFSETS, axis=-1)

        q_sb = q_sb.reshape(b, s, SB_HEADS, SB_HEAD_DIM)
        k_sb = k_sb.reshape(b, s, SB_HEADS, SB_HEAD_DIM)
        v_sb = v_sb.reshape(b, s, SB_HEADS, SB_HEAD_DIM)
        y_sb = stick_breaking_attention(q_sb, k_sb, v_sb).reshape(b, s, SB_WIDTH) @ w_o_sb[l]

        q = (rms_norm(q_lat, q_norm[l]) @ w_uq[l]).reshape(b, s, MLA_HEADS, MLA_QK_DIM)
        q_nope, q_rope = q[..., :MLA_NOPE_DIM], rope(q[..., MLA_NOPE_DIM:], positions)
        kv = (rms_norm(kv_lat, kv_norm[l]) @ w_ukv[l]).reshape(b, s, MLA_HEADS, MLA_NOPE_DIM + MLA_V_DIM)
        k_nope, v_mla = kv[..., :MLA_NOPE_DIM], kv[..., MLA_NOPE_DIM:]
        k_rope = rope(k_r, positions)
        y_mla = mla_attention(q_nope, q_rope, k_nope, k_rope, v_mla).reshape(b, s, MLA_HEADS * MLA_V_DIM) @ w_o_mla[l]

        gates = jax.nn.sigmoid((gate_logits + b_gate[l]).astype(jnp.float32)).astype(h.dtype)
        g_sb, g_mla = gates[..., :D_MODEL], gates[..., D_MODEL:]
        mixed = (g_sb * y_sb + g_mla * y_mla) @ w_out[l]
        h = h + rms_norm(mixed, norm_post_mix[l])

        gu = rms_norm(h, norm_pre_ffn[l]) @ w_gate_up[l]
        g, up = gu[..., :D_FF], gu[..., D_FF:]
        h = h + rms_norm((jax.nn.silu(g) * up) @ w_down[l], norm_post_ffn[l])
    return h
```

```python
import concourse.bass as bass
import concourse.mybir as mybir

class Op:
    __slots__ = ("eng", "fn", "waits", "need", "sig", "dma_sem", "dma_val", "idx")
    def __init__(self, eng, fn, waits):
        self.eng = eng; self.fn = fn; self.waits = [w for w in waits if w is not None]
        self.need = False; self.sig = None; self.dma_sem = None; self.dma_val = None

class Prog:
    ENGS = ("pe", "act", "dve", "pool", "sp")
    def __init__(self, nc, stack):
        self.nc = nc
        self.stack = stack
        self.ops = {e: [] for e in self.ENGS}
        self.esem = {e: stack.enter_context(nc.semaphore("s_" + e)) for e in self.ENGS}
        self.ecount = {e: 0 for e in self.ENGS}
        self.dsem = {}
        self.dcount = {}
        self.seen = {e: {} for e in self.ENGS}
        self.n_instr = 0

    def add(self, eng, fn, waits=()):
        op = Op(eng, fn, waits)
        for w in op.waits:
            if w.dma_sem is None:
                w.need = True
        self.ops[eng].append(op)
        return op

    def dma(self, eng, fn, key, waits=()):
        op = Op(eng, fn, waits)
        for w in op.waits:
            if w.dma_sem is None:
                w.need = True
        if key not in self.dsem:
            self.dsem[key] = self.stack.enter_context(self.nc.semaphore("d_" + key))
            self.dcount[key] = 0
        self.dcount[key] += 16
        op.dma_sem = key; op.dma_val = self.dcount[key]
        self.ops[eng].append(op)
        return op

    def flush(self):
        nc = self.nc
        for e in self.ENGS:
            for op in self.ops[e]:
                if op.dma_sem is None and op.need:
                    self.ecount[e] += 1
                    op.sig = self.ecount[e]
        def replay(ename):
            def run(eng):
                seen = self.seen[ename]
                for op in self.ops[ename]:
                    for w in op.waits:
                        if w.dma_sem is not None:
                            k = "d_" + w.dma_sem; sem = self.dsem[w.dma_sem]; val = w.dma_val
                        else:
                            if w.eng == ename and False:
                                continue
                            k = "e_" + w.eng; sem = self.esem[w.eng]; val = w.sig
                            assert val is not None
                        if seen.get(k, 0) >= val:
                            continue
                        seen[k] = val
                        eng.wait_ge(sem, val)
                        self.n_instr += 1
                    ins = op.fn(eng)
                    self.n_instr += 1
                    if op.dma_sem is not None:
                        ins.then_inc(self.dsem[op.dma_sem], 16)
                    elif op.need:
                        ins.then_inc(self.esem[ename], 1)
            return run
        with nc.Block() as block:
            block.tensor(replay("pe"))
            block.scalar(replay("act"))
            block.vector(replay("dve"))
            block.gpsimd(replay("pool"))
            block.sync(replay("sp"))
        self.ops = {e: [] for e in self.ENGS}

import contextlib, math
import numpy as np
import concourse.bass as bass
import concourse.mybir as mybir
F32 = mybir.dt.float32; BF16 = mybir.dt.bfloat16; I32 = mybir.dt.int32
AF = mybir.ActivationFunctionType; ALU = mybir.AluOpType
EPS = 1e-6
TWO_PI = 2.0 * math.pi
C1 = 6.28125; C2 = float(np.float32(TWO_PI - 6.28125)); C3 = float(TWO_PI - 6.28125 - np.float64(np.float32(TWO_PI - 6.28125)))

def build_l1(S, phaseB=True, dbg=False):
    TA = 256; NTA = S // TA; NB = S // 128; NQT = S // 512
    nc = bass.Bass("TRN2", target_bir_lowering=False)
    D = lambda name, shape, dt, kind="ExternalInput": nc.dram_tensor(name, shape, dt, kind=kind).ap()
    xT = D("xT", [2048, S], F32)
    pos = D("pos", [128, S], I32)
    w1 = D("w1", [2048, 2304], F32)
    wuq = D("wuq", [768, 512], F32)
    wukv = D("wukv", [512, 512], F32)
    gains = D("gains", [128, 26], F32)
    cmat = D("cmat", [128, 6, 128], F32)
    masks = D("masks", [128, 8, 512], F32)
    invf = D("invf", [128, 1], F32)
    FT = D("FT", [10, 128, S], BF16, "ExternalOutput" if dbg else "Internal")
    Vs = D("Vs", [4, 128, NB, 128], BF16, "ExternalOutput" if dbg else "Internal")
    OT = D("OT", [4, 128, S], BF16, "ExternalOutput")
    with contextlib.ExitStack() as st0:
        P = Prog(nc, st0)
        T0 = lambda name, shape, dt: st0.enter_context(nc.sbuf_tensor(name, shape, dt))
        cm = T0("cm", [128, 6, 128], BF16)
        gn = T0("gn", [128, 26], F32)
        ivf = T0("ivf", [128, 1], F32)
        cst = T0("cst", [128, 4], F32)
        ld_c = [P.dma("pool", lambda e: e.dma_start(out=cm[:], in_=cmat[:, :, :]), "c"),
                P.dma("sp", lambda e: e.dma_start(out=gn[:], in_=gains[:, :]), "c"),
                P.dma("sp", lambda e: e.dma_start(out=ivf[:], in_=invf[:, :]), "c")]
        ms = [P.add("pool", lambda e: e.memset(cst[:, 0:1], EPS)),
              P.add("pool", lambda e: e.memset(cst[:, 1:2], 1.0))]
        CONST = ld_c + ms
        ident = cm[:, 0, :]; tri = cm[:, 1, :]; c2048 = cm[:, 2, :]; c768 = cm[:, 3, :]; c512 = cm[:, 4, :]; ones = cm[:, 5, :]
        with contextlib.ExitStack() as st:
            T = lambda name, shape, dt: st.enter_context(nc.sbuf_tensor(name, shape, dt))
            stA = [T(f"stA{i}", [128, 6, TA], BF16) for i in range(2)]
            W1 = T("W1", [128, 16, 2304], BF16)
            WQ = T("WQ", [128, 6, 512], BF16)
            WKV = T("WKV", [128, 4, 512], BF16)
            xf = T("xf", [128, 16, TA], F32)
            sq = T("sq", [128, 16, TA], BF16)
            xb = [T(f"xb{i}", [128, 16, TA], BF16) for i in range(2)]
            stB = [T(f"stB{i}", [128, 8, TA], BF16) for i in range(2)]
            ql = T("ql", [128, 6, TA], BF16); sqq = T("sqq", [128, 6, TA], BF16); qn2 = [T(f"qn{i}", [128, 6, TA], BF16) for i in range(2)]
            kvl = T("kvl", [128, 4, TA], BF16); sqkv = T("sqkv", [128, 4, TA], BF16); kvn2 = [T(f"kvn{i}", [128, 4, TA], BF16) for i in range(2)]
            rs = [T(f"rs{i}", [128, 3, TA], F32) for i in range(2)]
            lnt = T("lnt", [128, 3, TA], F32)
            pi = [T(f"pi{i}", [128, TA], I32) for i in range(2)]
            rp = T("rp", [128, 8, TA], F32)
            ki = T("ki", [128, TA], I32)
            cs = [T(f"cs{i}", [128, 2, TA], F32) for i in range(3)]
            vtok = [T(f"vtok{i}", [128, 8, 128], BF16) for i in range(2)]
            psf = st.enter_context(nc.psum_tensor("psf", [128, 7, 512], F32))
            pst = st.enter_context(nc.psum_tensor("pst", [128, 1024], BF16))
            wl = []
            for k in range(16):
                for c0 in range(0, 2304, 1152):
                    wl.append(P.dma("pool", lambda e, k=k, c0=c0: e.dma_start(out=W1[:, k, c0:c0 + 1152], in_=w1[k * 128:(k + 1) * 128, c0:c0 + 1152]), "w"))
            for k in range(6):
                wl.append(P.dma("pool", lambda e, k=k: e.dma_start(out=WQ[:, k, :], in_=wuq[k * 128:(k + 1) * 128, :]), "w"))
            for k in range(4):
                wl.append(P.dma("pool", lambda e, k=k: e.dma_start(out=WKV[:, k, :], in_=wukv[k * 128:(k + 1) * 128, :]), "w"))
            ng = []
            for base in (2176, 2240):
                ng.append(P.add("dve", lambda e, b=base: e.tensor_scalar(out=W1[:, :, b:b + 32], in0=W1[:, :, b:b + 32], scalar1=-1.0, scalar2=None, op0=ALU.mult), wl))
            for base in (384, 448):
                ng.append(P.add("dve", lambda e, b=base: e.tensor_scalar(out=WQ[:, :, b:b + 32], in0=WQ[:, :, b:b + 32], scalar1=-1.0, scalar2=None, op0=ALU.mult), wl))
            WREADY = wl + ng + CONST
            NS1 = 3; NS2 = 2
            def slot1(i): i %= NS1; return psf[:, 2 + i, 0:256]
            def slot2(i): i %= NS2; return psf[:, 5 + i, 0:256]
            s1_free = [None] * NS1; s2_free = [None] * NS2
            s1_cnt = [0]; s2_cnt = [0]
            stat_free = [None, None]
            stat_cnt = [0]
            def statslot():
                i = stat_cnt[0] % 2; stat_cnt[0] += 1
                return i, psf[:, i, 0:256]
            st_ = {}
            last = {"norm": [], "sqr": None, "ssx": None, "s1": {}, "s2": {}, "stA_store": {}, "stB_store": {}, "vstore": {}, "tr_evac": None,
                    "sqq_read": None, "sqkv_read": None, "qn_read": None, "kvn_read": None, "ql_w": None, "cs_read": {}, "rs_read": {}}
            stores = []

            def rstd_chain(ps_ap, slot_i, out_ap, idx, waits):
                a = P.add("act", lambda e: e.activation(out=lnt[:, idx, :], in_=ps_ap, func=AF.Ln, bias=cst[:, 0:1]), waits)
                b = P.add("act", lambda e: e.activation(out=out_ap, in_=lnt[:, idx, :], func=AF.Exp, scale=-0.5))
                stat_free[slot_i] = a
                return b

            def front(t):
                d = st_.setdefault(t, {})
                tok = slice(t * TA, (t + 1) * TA)
                ld = P.dma("sp", lambda e: e.dma_start(out=xf[:], in_=xT[:, tok].rearrange("(k p) t -> p k t", p=128)), "x", last["norm"] + [last["sqr"]])
                ldp = P.dma("sp", lambda e: e.dma_start(out=pi[t % 2][:], in_=pos[:, tok]), "p%d" % (t % 2), [st_.get(t - 2, {}).get("posf")])
                sqr = P.add("act", lambda e: e.activation(out=sq[:], in_=xf[:], func=AF.Square), [ld, last["ssx"]])
                last["sqr"] = sqr
                si, sap = statslot()
                mm = None
                for k in range(16):
                    mm = P.add("pe", lambda e, k=k: e.matmul(sap, lhsT=c2048, rhs=sq[:, k, :], start=(k == 0), stop=(k == 15)),
                               [sqr, stat_free[si]] + (CONST if t == 0 else []))
                last["ssx"] = mm
                r = rs[t % 2]
                rr = rstd_chain(sap, si, r[:, 0, :], 0, [mm] + last["rs_read"].get(t % 2, []))
                nm = []
                prev_s1 = last["s1"].get(t - 2)
                for k in range(16):
                    eng = "dve"
                    nm.append(P.add(eng, lambda e, k=k: e.scalar_tensor_tensor(out=xb[t % 2][:, k, :], in0=xf[:, k, :], scalar=gn[:, k:k + 1], in1=r[:, 0, :],
                                                                               op0=ALU.mult, op1=ALU.mult), [rr, prev_s1, ld]))
                last["norm"] = nm[-2:]
                d["norm"] = nm[-2:]
                c = cs[t % 3]
                w0 = last["cs_read"].get(t % 3, [])
                o = P.add("dve", lambda e: e.tensor_copy(out=rp[:, 0, :], in_=pi[t % 2][:]), [ldp])
                d["posf"] = o
                P.add("dve", lambda e: e.tensor_scalar(out=rp[:, 1, :], in0=rp[:, 0, :], scalar1=ivf[:, 0:1], scalar2=None, op0=ALU.mult))
                P.add("dve", lambda e: e.tensor_scalar(out=ki[:], in0=rp[:, 1, :], scalar1=1.0 / TWO_PI, scalar2=None, op0=ALU.mult))
                P.add("dve", lambda e: e.tensor_copy(out=rp[:, 2, :], in_=ki[:]))
                P.add("dve", lambda e: e.scalar_tensor_tensor(out=rp[:, 3, :], in0=rp[:, 2, :], scalar=-C1, in1=rp[:, 1, :], op0=ALU.mult, op1=ALU.add), [last.get("sh")])
                P.add("dve", lambda e: e.scalar_tensor_tensor(out=rp[:, 3, :], in0=rp[:, 2, :], scalar=-C2, in1=rp[:, 3, :], op0=ALU.mult, op1=ALU.add))
                P.add("dve", lambda e: e.scalar_tensor_tensor(out=rp[:, 3, :], in0=rp[:, 2, :], scalar=-C3, in1=rp[:, 3, :], op0=ALU.mult, op1=ALU.add))
                rcl = P.add("dve", lambda e: e.tensor_scalar(out=rp[:, 3, :], in0=rp[:, 3, :], scalar1=math.pi, scalar2=-math.pi, op0=ALU.min, op1=ALU.max))
                sn = P.add("act", lambda e: e.activation(out=c[:, 1, :], in_=rp[:, 3, :], func=AF.Sin), [rcl] + w0)
                sh = P.add("act", lambda e: e.activation(out=rp[:, 4, :], in_=rp[:, 3, :], func=AF.Sin, scale=0.5), [last.get("shsq")])
                last["sh"] = sh
                last["shsq"] = P.add("dve", lambda e: e.tensor_tensor(out=rp[:, 5, :], in0=rp[:, 4, :], in1=rp[:, 4, :], op=ALU.mult), [sh])
                csr = P.add("dve", lambda e: e.tensor_scalar(out=c[:, 0, :], in0=rp[:, 5, :], scalar1=-2.0, scalar2=1.0, op0=ALU.mult, op1=ALU.add), w0)
                d["cs"] = [sn, csr]

            def mgroup(slot_ap, free_op, lhs_fn, rhs_fn, nk, waits):
                mm = None
                for k in range(nk):
                    mm = P.add("pe", lambda e, k=k: e.matmul(slot_ap, lhsT=lhs_fn(k), rhs=rhs_fn(k), start=(k == 0), stop=(k == nk - 1)),
                               (waits + [free_op]) if k == 0 else [])
                return mm

            def stage1(t):
                d = st_[t]
                qn = qn2[t % 2]; kvn = kvn2[t % 2]
                X = xb[t % 2]
                wts = d["norm"] + (WREADY if t == 0 else [])
                A = stA[t % 2]
                mm_last = None
                evs = []
                for j in range(6):
                    i = s1_cnt[0]; s1_cnt[0] += 1
                    ap = slot1(i)
                    import os
                    EXP = os.environ.get("EXP", "")
                    mm = mgroup(ap, s1_free[i % NS1], (lambda k: c2048) if EXP == "b" else (lambda k, j=j: W1[:, k, j * 128:(j + 1) * 128]), lambda k: X[:, k, :], 16, wts)
                    ev = P.add("act", (lambda e: e.activation(out=lnt[:, 0, 0:1], in_=cst[:, 1:2], func=AF.Copy)) if EXP == "a" else (lambda e, ap=ap, j=j: e.activation(out=A[:, j, :], in_=ap, func=AF.Copy)),
                               [mm, last["stA_store"].get(t % 2), last["tr_in"].get(t % 2) if "tr_in" in last else None])
                    s1_free[i % NS1] = ev; evs.append(ev); mm_last = mm
                d["A_ev"] = evs
                import os
                SUB = int(os.environ.get("SUB", "9"))
                if SUB <= 1:
                    last["s1"][t] = mm_last; return
                qe = []
                for j in range(6):
                    i = s1_cnt[0]; s1_cnt[0] += 1
                    ap = slot1(i)
                    mm = mgroup(ap, s1_free[i % NS1], lambda k, j=j: W1[:, k, 768 + j * 128:768 + (j + 1) * 128], lambda k: X[:, k, :], 16, wts)
                    e1 = P.add("dve", lambda e, ap=ap, j=j: e.tensor_copy(out=ql[:, j, :], in_=ap), [mm, last["qn_w"] if "qn_w" in last else None])
                    e2 = P.add("act", lambda e, ap=ap, j=j: e.activation(out=sqq[:, j, :], in_=ql[:, j, :], func=AF.Square), [e1, last["sqq_read"]])
                    s1_free[i % NS1] = e1
                    qe.append((e1, e2)); mm_last = mm
                    s1_free[i % NS1] = e1
                si, sap = statslot()
                mm = None
                for k in range(6):
                    mm = P.add("pe", lambda e, k=k, sap=sap: e.matmul(sap, lhsT=c768, rhs=sqq[:, k, :], start=(k == 0), stop=(k == 5)), [qe[k][1], stat_free[si]])
                last["sqq_read"] = mm
                r = rs[t % 2]
                rq = rstd_chain(sap, si, r[:, 1, :], 1, [mm])
                qnw = None
                for k in range(6):
                    qnw = P.add("dve", lambda e, k=k: e.scalar_tensor_tensor(out=qn[:, k, :], in0=ql[:, k, :], scalar=gn[:, 16 + k:17 + k], in1=r[:, 1, :], op0=ALU.mult, op1=ALU.mult),
                                [rq, qe[k][0], last["qn_read"]])
                last["qn_w"] = qnw; d["qn"] = qnw
                if SUB <= 2:
                    last["s1"][t] = mm_last; return
                ke = []
                for j in range(4):
                    i = s1_cnt[0]; s1_cnt[0] += 1
                    ap = slot1(i)
                    mm = mgroup(ap, s1_free[i % NS1], lambda k, j=j: W1[:, k, 1536 + j * 128:1536 + (j + 1) * 128], lambda k: X[:, k, :], 16, wts)
                    e1 = P.add("dve", lambda e, ap=ap, j=j: e.tensor_copy(out=kvl[:, j, :], in_=ap), [mm, last["kvn_w"] if "kvn_w" in last else None])
                    e2 = P.add("act", lambda e, ap=ap, j=j: e.activation(out=sqkv[:, j, :], in_=kvl[:, j, :], func=AF.Square), [e1, last["sqkv_read"]])
                    ke.append((e1, e2))
                    s1_free[i % NS1] = e1
                si, sap = statslot()
                for k in range(4):
                    mm = P.add("pe", lambda e, k=k, sap=sap: e.matmul(sap, lhsT=c512, rhs=sqkv[:, k, :], start=(k == 0), stop=(k == 3)), [ke[k][1], stat_free[si]])
                last["sqkv_read"] = mm
                rk = rstd_chain(sap, si, r[:, 2, :], 2, [mm])
                kw = None
                for k in range(4):
                    kw = P.add("dve", lambda e, k=k: e.scalar_tensor_tensor(out=kvn[:, k, :], in0=kvl[:, k, :], scalar=gn[:, 22 + k:23 + k], in1=r[:, 2, :], op0=ALU.mult, op1=ALU.mult),
                               [rk, ke[k][0], last["kvn_read"]])
                last["kvn_w"] = kw; d["kvn"] = kw
                last["rs_read"][t % 2] = [qnw, kw]
                if SUB <= 3:
                    last["s1"][t] = mm_last; return
                B = stB[t % 2]
                c = cs[t % 3]
                aps = []
                for j in range(2):
                    i = s1_cnt[0]; s1_cnt[0] += 1
                    ap = slot1(i)
                    mm = mgroup(ap, s1_free[i % NS1], lambda k, j=j: W1[:, k, 2048 + j * 128:2048 + (j + 1) * 128], lambda k: X[:, k, :], 16, wts)
                    aps.append((ap, mm, i)); mm_last = mm
                t1 = P.add("dve", lambda e: e.tensor_tensor(out=rp[:, 6, :], in0=aps[0][0], in1=c[:, 0, :], op=ALU.mult), [aps[0][1]] + d["cs"])
                t2 = P.add("dve", lambda e: e.tensor_tensor(out=rp[:, 7, :], in0=aps[1][0], in1=c[:, 1, :], op=ALU.mult), [aps[1][1]])
                kr = P.add("dve", lambda e: e.tensor_tensor(out=B[:, 5, :], in0=rp[:, 6, :], in1=rp[:, 7, :], op=ALU.add), [last["stB_store"].get(t % 2)])
                s1_free[aps[0][2] % NS1] = t1; s1_free[aps[1][2] % NS1] = t2
                d["kr"] = kr
                last["s1"][t] = mm_last
                if SUB <= 4: return
                tok = slice(t * TA, (t + 1) * TA)
                so = P.dma("sp", lambda e: e.dma_start(out=FT[0:4, :, tok].rearrange("f p t -> p f t"), in_=A[:, 0:4, :]), "sa%d" % (t % 2), evs[0:4])
                stores.append(so); d["stA_store"] = so

            def stage2(t):
                d = st_[t]
                qn = qn2[t % 2]; kvn = kvn2[t % 2]
                B = stB[t % 2]; A = stA[t % 2]; c = cs[t % 3]
                tok = slice(t * TA, (t + 1) * TA)
                evs = []
                def grp(lhs_fn, rhs_fn, nk, waits):
                    i = s2_cnt[0]; s2_cnt[0] += 1
                    ap = slot2(i)
                    mm = mgroup(ap, s2_free[i % NS2], lhs_fn, rhs_fn, nk, waits)
                    return ap, mm, i % NS2
                wq = [d["qn"]]; wk = [d["kvn"]]
                bst = [last["stB_store"].get(t % 2), last["tr_in"].get(t % 2) if "tr_in" in last else None]
                for h in range(2):
                    ap, mm, i = grp(lambda k, h=h: WQ[:, k, h * 128:(h + 1) * 128], lambda k: qn[:, k, :], 6, wq)
                    ev = P.add("act", lambda e, ap=ap, h=h: e.activation(out=B[:, h, :], in_=ap, func=AF.Copy), [mm] + bst)
                    s2_free[i] = ev; evs.append(ev)
                apA, mmA, iA = grp(lambda k: WQ[:, k, 256:384], lambda k: qn[:, k, :], 6, wq)
                apB, mmB, iB = grp(lambda k: WQ[:, k, 384:512], lambda k: qn[:, k, :], 6, wq)
                last["qn_read"] = mmB
                t1 = P.add("dve", lambda e: e.tensor_tensor(out=rp[:, 6, :], in0=apA, in1=c[:, 0, :], op=ALU.mult), [mmA])
                t2 = P.add("dve", lambda e: e.tensor_tensor(out=rp[:, 7, :], in0=apB, in1=c[:, 1, :], op=ALU.mult), [mmB])
                qr = P.add("dve", lambda e: e.tensor_tensor(out=B[:, 2, :], in0=rp[:, 6, :], in1=rp[:, 7, :], op=ALU.add), bst)
                s2_free[iA] = t1; s2_free[iB] = t2
                last["cs_read"][t % 3] = [t1, t2]
                evs.append(qr)
                for h in range(4):
                    ap, mm, i = grp(lambda k, h=h: WKV[:, k, h * 128:(h + 1) * 128], lambda k: kvn[:, k, :], 4, wk)
                    dst = (3 + h) if h < 2 else (6 + h - 2)
                    ev = P.add("act", lambda e, ap=ap, dst=dst: e.activation(out=B[:, dst, :], in_=ap, func=AF.Copy), [mm] + bst)
                    s2_free[i] = ev; evs.append(ev)
                    if h == 3: last["kvn_read"] = mm
                evs.append(d["kr"])
                so = P.dma("sp", lambda e: e.dma_start(out=FT[4:10, :, tok].rearrange("f p t -> p f t"), in_=B[:, 0:6, :]), "sb%d" % (t % 2), evs)
                stores.append(so); last["stB_store"][t % 2] = so
                srcs = [A[:, 4, :], A[:, 5, :], B[:, 6, :], B[:, 7, :]]
                tr = None
                for h4 in range(4):
                    for j in range(2):
                        tr = P.add("pe", lambda e, h4=h4, j=j: e.transpose(pst[:, (h4 * 2 + j) * 128:(h4 * 2 + j + 1) * 128], srcs[h4][:, j * 128:(j + 1) * 128], ident),
                                   d["A_ev"][4:6] + evs[5:7] + [last["tr_evac"]])
                last.setdefault("tr_in", {})[t % 2] = tr
                vt = vtok[t % 2]
                ev = P.add("act", lambda e: e.activation(out=vt[:].rearrange("p a d -> p (a d)"), in_=pst[:, :], func=AF.Copy), [tr, last["vstore"].get(t % 2)])
                last["tr_evac"] = ev
                so2 = P.dma("sp", lambda e: e.dma_start(out=Vs[:, :, 2 * t:2 * t + 2, :].rearrange("h p j d -> p h j d"), in_=vt[:].rearrange("p (h j) d -> p h j d", j=2)), "sv%d" % (t % 2), [ev])
                stores.append(so2); last["vstore"][t % 2] = so2
                last["stA_store"][t % 2] = d["stA_store"]

            import os
            CUT = int(os.environ.get("CUT", "9"))
            if CUT >= 2: front(0)
            for t in range(NTA + 1):
                if CUT < 2: break
                if t + 1 < NTA: front(t + 1)
                if t < NTA and CUT >= 3: stage1(t)
                if t >= 1 and CUT >= 4: stage2(t - 1)
            if CUT < 9:
                P.add("dve", lambda e: e.tensor_copy(out=cst[:, 3:4], in_=cst[:, 1:2]), WREADY)
            barA = P.add("sp", lambda e: e.wait_ge(P.esem["sp"], 0), stores + last["norm"])
            barA.need = True
            P.flush()
        print("phase A instrs", P.n_instr, flush=True)
        if not phaseB:
            return nc
        with contextlib.ExitStack() as st:
            T = lambda name, shape, dt: st.enter_context(nc.sbuf_tensor(name, shape, dt))
            KT = T("KT", [128, S], BF16)
            KR = T("KR", [128, S], BF16)
            V = T("V", [128, NB, 128], BF16)
            qb = [T(f"qb{i}", [128, 2, 512], BF16) for i in range(2)]
            u = [T(f"u{i}", [128, 512], F32) for i in range(3)]
            sp_ = [T(f"sp{i}", [128, 512], BF16) for i in range(3)]
            ec = [T(f"ec{i}", [128, 512], F32) for i in range(2)]
            w = [T(f"wt{i}", [128, 512], BF16) for i in range(3)]
            Sb = [T(f"Sb{i}", [128, 512], BF16) for i in range(3)]
            Sf = [T(f"Sf{i}", [128, 512], F32) for i in range(3)]
            ost = [T(f"ost{i}", [128, 512], BF16) for i in range(2)]
            rec = T("rec", [128, 512], F32)
            mk = T("mk", [128, 8, 512], BF16)
            for jj in range(8):
                mkl = P.dma("pool", lambda e, jj=jj: e.dma_start(out=mk[:, jj, :], in_=masks[:, jj, :]), "c", [barA])
            ps = st.enter_context(nc.psum_tensor("psB", [128, 8, 512], F32))
            Sbank = [ps[:, 0, :], ps[:, 1, :]]; Cbank = [ps[:, 2, :], ps[:, 3, :]]; Obank = [ps[:, 4, :], ps[:, 5, :]]; Dbank = [ps[:, 6, :], ps[:, 7, :]]
            KCH = min(2048, S); NKC = S // KCH
            VCH = KCH // 128
            prev_head_S = [None]; prev_head_O = [None]
            out_stores = []
            ost_store = [None, None]
            obank_free = [None, None]
            qb_free = [None, None]
            scale_sb = 128 ** -0.5; scale_mla = 192 ** -0.5
            krl = None
            for hd in range(4):
                mla = hd >= 2; h = hd % 2
                fk = (2 + h) if not mla else (7 + h)
                fq = h if not mla else (4 + h)
                kl = [P.dma("sp", lambda e, c=c, fk=fk: e.dma_start(out=KT[:, c * KCH:(c + 1) * KCH], in_=FT[fk, :, c * KCH:(c + 1) * KCH]), "k%d" % c, [prev_head_S[0], barA]) for c in range(NKC)]
                vl = [P.dma("sp", lambda e, c=c, hd=hd: e.dma_start(out=V[:, c * VCH:(c + 1) * VCH, :], in_=Vs[hd, :, c * VCH:(c + 1) * VCH, :]), "v%d" % c, [prev_head_O[0], barA]) for c in range(NKC)]
                if hd == 2:
                    krl = [P.dma("sp", lambda e, c=c: e.dma_start(out=KR[:, c * KCH:(c + 1) * KCH], in_=FT[9, :, c * KCH:(c + 1) * KCH]), "kr%d" % c, [barA]) for c in range(NKC)]
                steps = []
                for qt in range(NQT):
                    nkb = 4 * qt + 4
                    for n, kb in enumerate(range(nkb - 1, -1, -1)):
                        steps.append(dict(qt=qt, kb=kb, first=(n == 0), last=(n == nkb - 1), j=(kb - 4 * qt) if kb >= 4 * qt else None))
                n = len(steps)
                R = {}
                qload = {}
                def get_q(qt):
                    if qt not in qload:
                        q = qb[qt % 2]
                        l = [P.dma("sp", lambda e, fq=fq: e.dma_start(out=q[:, 0, :], in_=FT[fq, :, qt * 512:(qt + 1) * 512]), "qa%d" % (qt % 2), [qb_free[qt % 2], barA])]
                        if mla:
                            l.append(P.dma("sp", lambda e: e.dma_start(out=q[:, 1, :], in_=FT[6, :, qt * 512:(qt + 1) * 512]), "qb%d" % (qt % 2), [qb_free[qt % 2], barA]))
                        qload[qt] = l
                    return qload[qt]
                for it in range(n + 3):
                    if it < n and steps[it]["first"]:
                        get_q(steps[it]["qt"])
                        if steps[it]["qt"] + 1 < NQT: get_q(steps[it]["qt"] + 1)
                    if it < n:
                        s = steps[it]; r = R.setdefault(it, {})
                        q = qb[s["qt"] % 2]; kb = s["kb"]
                        ksl = slice(kb * 128, (kb + 1) * 128)
                        wts = [kl[kb * 128 // KCH]] + get_q(s["qt"]) + [R.get(it - 2, {}).get("exp")]
                        if not mla:
                            r["S"] = P.add("pe", lambda e, q=q, ksl=ksl, it=it: e.matmul(Sbank[it % 2], lhsT=KT[:, ksl], rhs=q[:, 0, :], start=True, stop=True), wts)
                        else:
                            P.add("pe", lambda e, q=q, ksl=ksl, it=it: e.matmul(Sbank[it % 2], lhsT=KT[:, ksl], rhs=q[:, 0, :], start=True, stop=False), wts + [krl[kb * 128 // KCH]])
                            r["S"] = P.add("pe", lambda e, q=q, ksl=ksl, it=it, h=h: e.matmul(Sbank[it % 2], lhsT=KR[64 * h:64 * h + 64, ksl], rhs=q[64 * h:64 * h + 64, 1, :], start=False, stop=True))
                        if s["last"]:
                            qb_free[s["qt"] % 2] = r["S"]
                        prev_head_S[0] = r["S"]
                    if not mla:
                        if it < n:
                            s = steps[it]; r = R[it]
                            r["exp"] = P.add("act", lambda e, it=it: e.activation(out=u[it % 3][:], in_=Sbank[it % 2], func=AF.Exp, scale=scale_sb), [r["S"], R.get(it - 3, {}).get("w")])
                            r["ln"] = P.add("act", lambda e, it=it: e.activation(out=sp_[it % 3][:], in_=u[it % 3][:], func=AF.Ln, bias=cst[:, 1:2]),
                                            [R.get(it - 3, {}).get("C"), R.get(it - 3, {}).get("Sb"), R.get(it - 3, {}).get("Sf")])
                            spr = r["ln"]
                            if s["j"] is not None:
                                spr = P.add("dve", lambda e, it=it, j=s["j"]: e.tensor_tensor(out=sp_[it % 3][:], in0=sp_[it % 3][:], in1=mk[:, j, :], op=ALU.mult), [r["ln"], mkl])
                            r["sp"] = spr
                            if s["first"]:
                                r["Sb"] = P.add("dve", lambda e, it=it: e.tensor_copy(out=Sb[it % 3][:], in_=sp_[it % 3][:]), [spr, R.get(it - 2, {}).get("C")])
                                r["Sf"] = P.add("pool", lambda e, it=it: e.tensor_copy(out=Sf[it % 3][:], in_=sp_[it % 3][:]), [spr, R.get(it - 2, {}).get("Sb")])
                            else:
                                r["Sb"] = P.add("dve", lambda e, it=it: e.tensor_tensor(out=Sb[it % 3][:], in0=Sf[(it - 1) % 3][:], in1=sp_[it % 3][:], op=ALU.add),
                                                [spr, R[it - 1]["Sf"], R.get(it - 2, {}).get("C")])
                                r["Sf"] = P.add("pool", lambda e, it=it: e.tensor_tensor(out=Sf[it % 3][:], in0=Sf[(it - 1) % 3][:], in1=sp_[it % 3][:], op=ALU.add),
                                                [spr, R.get(it - 2, {}).get("Sb")])
                        i1 = it - 1
                        if 0 <= i1 < n:
                            s = steps[i1]; r = R[i1]
                            wts = [r["sp"], R.get(i1 - 2, {}).get("expC")]
                            if s["first"]:
                                r["C"] = P.add("pe", lambda e, i1=i1: e.matmul(Cbank[i1 % 2], lhsT=tri, rhs=sp_[i1 % 3][:], start=True, stop=True), wts)
                            else:
                                P.add("pe", lambda e, i1=i1: e.matmul(Cbank[i1 % 2], lhsT=tri, rhs=sp_[i1 % 3][:], start=True, stop=False), wts)
                                r["C"] = P.add("pe", lambda e, i1=i1: e.matmul(Cbank[i1 % 2], lhsT=ones, rhs=Sb[(i1 - 1) % 3][:], start=False, stop=True), [R[i1 - 1]["Sb"]])
                            r["expC"] = P.add("act", lambda e, i1=i1: e.activation(out=ec[i1 % 2][:], in_=Cbank[i1 % 2], func=AF.Exp, scale=-1.0), [r["C"], R.get(i1 - 2, {}).get("w")])
                            wop = P.add("dve", lambda e, i1=i1: e.tensor_tensor(out=w[i1 % 3][:], in0=u[i1 % 3][:], in1=ec[i1 % 2][:], op=ALU.mult), [r["expC"], R.get(i1 - 3, {}).get("O")])
                            if s["j"] is not None:
                                wop = P.add("dve", lambda e, i1=i1, j=s["j"]: e.tensor_tensor(out=w[i1 % 3][:], in0=w[i1 % 3][:], in1=mk[:, j, :], op=ALU.mult), [mkl])
                            r["w"] = wop
                    else:
                        if it < n:
                            s = steps[it]; r = R[it]
                            pe = P.add("act", lambda e, it=it: e.activation(out=w[it % 3][:], in_=Sbank[it % 2], func=AF.Exp, scale=scale_mla), [r["S"], R.get(it - 3, {}).get("O")])
                            r["exp"] = pe
                            if s["j"] is not None:
                                pe = P.add("dve", lambda e, it=it, j=s["j"]: e.tensor_tensor(out=w[it % 3][:], in0=w[it % 3][:], in1=mk[:, 4 + j, :], op=ALU.mult), [pe, mkl])
                            r["w"] = pe
                    io = it - 2 if not mla else it - 1
                    if 0 <= io < n:
                        s = steps[io]; r = R[io]
                        qt = s["qt"]; kb = s["kb"]
                        wts = [r["w"], vl[kb * 128 // KCH]] + ([obank_free[qt % 2]] if s["first"] else [])
                        r["O"] = P.add("pe", lambda e, io=io, qt=qt, kb=kb, s=s: e.matmul(Obank[qt % 2], lhsT=V[:, kb, :], rhs=w[io % 3][:], start=s["first"], stop=s["last"]), wts)
                        if mla:
                            r["O"] = P.add("pe", lambda e, io=io, qt=qt, s=s: e.matmul(Dbank[qt % 2], lhsT=ones, rhs=w[io % 3][:], start=s["first"], stop=s["last"]))
                        prev_head_O[0] = r["O"]
                        if s["last"]:
                            o = ost[qt % 2]
                            if not mla:
                                ev = P.add("dve", lambda e, qt=qt, o=o: e.tensor_copy(out=o[:], in_=Obank[qt % 2]), [r["O"], ost_store[qt % 2]])
                            else:
                                rc = P.add("dve", lambda e, qt=qt: e.reciprocal(out=rec[:], in_=Dbank[qt % 2]), [r["O"]])
                                ev = P.add("dve", lambda e, qt=qt, o=o: e.tensor_tensor(out=o[:], in0=Obank[qt % 2], in1=rec[:], op=ALU.mult), [ost_store[qt % 2]])
                            obank_free[qt % 2] = ev
                            so = P.dma("sp", lambda e, qt=qt, o=o, hd=hd: e.dma_start(out=OT[hd, :, qt * 512:(qt + 1) * 512], in_=o[:]), "o%d" % (qt % 2), [ev])
                            ost_store[qt % 2] = so; out_stores.append(so)
            P.add("sp", lambda e: e.wait_ge(P.dsem["o0"], P.dcount["o0"]), out_stores[-2:])
            if "o1" in P.dsem:
                P.add("sp", lambda e: e.wait_ge(P.dsem["o1"], P.dcount["o1"]), [])
            P.flush()
        print("total instrs", P.n_instr, flush=True)
    return nc

def build_l2(NTOK, nffn=44):
    T = 512; NT = NTOK // T
    nc = bass.Bass("TRN2", target_bir_lowering=False)
    D = lambda name, shape, dt, kind="ExternalInput": nc.dram_tensor(name, shape, dt, kind=kind).ap()
    xT = D("xT", [2048, NTOK], F32)
    AT = D("AT", [16, 128, NTOK], BF16)
    WG = D("WG", [32, 128, 2048], F32); WO = D("WO", [32, 128, 1024], F32); WOUT = D("WOUT", [16, 128, 2048], F32)
    WGU = D("WGU", [88, 128, 2048], F32); WD = D("WD", [16, 128, 5632], F32)
    gains = D("gains2", [128, 96], F32)
    cmat = D("cmat2", [128, 128], F32)
    outT = D("outT", [2048, NTOK], F32, "ExternalOutput")
    WGb = D("WGb", [32, 128, 2048], BF16, "Internal"); WOb = D("WOb", [32, 128, 1024], BF16, "Internal"); WOUTb = D("WOUTb", [16, 128, 2048], BF16, "Internal")
    WGUb = D("WGUb", [88, 128, 2048], BF16, "Internal"); WDb = D("WDb", [16, 128, 5632], BF16, "Internal")
    with contextlib.ExitStack() as st:
        P = Prog(nc, st)
        Tn = lambda name, shape, dt: st.enter_context(nc.sbuf_tensor(name, shape, dt))
        B1 = Tn("B1", [128, 16, T], F32); B2 = Tn("B2", [128, 16, T], F32)
        B3 = Tn("B3", [128, 16, T], BF16); B4 = Tn("B4", [128, 44, T], BF16); B5 = Tn("B5", [128, 16, T], BF16)
        NSL = 6
        slots = [Tn(f"ws{i}", [128, 2048], BF16) for i in range(NSL)]
        tmp = [Tn(f"tmp{i}", [128, 4, T], F32) for i in range(2)]
        gn = Tn("gn", [128, 96], F32); c2048 = Tn("c2048", [128, 128], BF16); cst = Tn("cst", [128, 2], F32)
        rsd = Tn("rsd", [128, T], F32); lnt = Tn("lnt", [128, T], F32)
        ps = st.enter_context(nc.psum_tensor("ps", [128, 8, 512], F32))
        CONST = [P.dma("sp", lambda e: e.dma_start(out=gn[:], in_=gains[:, :]), "c"),
                 P.dma("pool", lambda e: e.dma_start(out=c2048[:], in_=cmat[:, :]), "c"),
                 P.add("dve", lambda e: e.memset(cst[:, 0:1], EPS))]
        cv = {}
        def conv(key, dst, src):
            cv[key] = P.dma("pool", lambda e: e.dma_start(out=dst, in_=src), "cv" + key)
        for m in range(32): conv("G", WGb[m, :, :], WG[m, :, :])
        for m in range(32): conv("O", WOb[m, :, :], WO[m, :, :])
        for m in range(16): conv("T", WOUTb[m, :, :], WOUT[m, :, :])
        for m in range(88): conv("U", WGUb[m, :, :], WGU[m, :, :])
        for m in range(16):
            for c0, c1 in ((0, 2048), (2048, 4096), (4096, 5632)):
                conv("D", WDb[m, :, c0:c1], WD[m, :, c0:c1])
        NBK = 6
        bank_free = [None] * NBK; bank_cnt = [0]
        slot_free = [None] * NSL; slot_cnt = [0]
        stat_free = [None]
        S = {"first": True}
        def wload(src_ap, ncols, cvkey):
            i = slot_cnt[0] % NSL; slot_cnt[0] += 1
            sl = slots[i]
            op = P.dma("sp", lambda e: e.dma_start(out=sl[:, 0:ncols], in_=src_ap), "w%d" % i, [slot_free[i], cv[cvkey]])
            return i, sl, op
        def group(parts, rhs_fn, waits):
            b = bank_cnt[0] % NBK; bank_cnt[0] += 1
            bank = ps[:, 1 + b, :]
            kg = 0; mm = None; nk = sum(kc for _, kc, _ in parts); first = True
            for src, kc, cvkey in parts:
                i, sl, ld = wload(src, kc * 128, cvkey)
                for k in range(kc):
                    w_ = [ld] if k == 0 else []
                    if first:
                        w_ = w_ + waits + [bank_free[b]] + (CONST if S["first"] else [])
                        first = False; S["first"] = False
                    mm = P.add("pe", lambda e, k=k, kg=kg, sl=sl: e.matmul(bank, lhsT=sl[:, k * 128:(k + 1) * 128], rhs=rhs_fn(kg), start=(kg == 0), stop=(kg == nk - 1)), w_)
                    kg += 1
                slot_free[i] = mm
            return b, bank, mm
        def stats(sq_fn, waits, out_ap):
            mm = None
            for k in range(16):
                mm = P.add("pe", lambda e, k=k: e.matmul(ps[:, 0, :], lhsT=c2048[:], rhs=sq_fn(k), start=(k == 0), stop=(k == 15)), (waits + [stat_free[0]] + CONST) if k == 0 else [])
            a = P.add("act", lambda e: e.activation(out=lnt[:], in_=ps[:, 0, :], func=AF.Ln, bias=cst[:, 0:1]), [mm])
            stat_free[0] = a
            b = P.add("act", lambda e: e.activation(out=out_ap, in_=lnt[:], func=AF.Exp, scale=-0.5), waits)
            return mm, b
        prev = {"store": None, "lastE": None, "gu_last": None, "wout_last": None}
        all_stores = []
        for t in range(NT):
            tok = slice(t * T, (t + 1) * T)
            xsrc = xT[:, tok].rearrange("(k p) t -> p k t", p=128)
            ldx = P.dma("sp", lambda e, xsrc=xsrc: e.dma_start(out=B1[:], in_=xsrc), "x", [prev["store"]])
            lda = P.dma("sp", lambda e, tok=tok: e.dma_start(out=B5[:], in_=AT[:, :, tok].rearrange("c p t -> p c t")), "a", [prev["wout_last"]])
            sq = B4[:, 16:32, :]
            sqr = P.add("act", lambda e: e.activation(out=sq, in_=B1[:], func=AF.Square), [ldx, prev["lastE"]])
            smm, r0 = stats(lambda k: B4[:, 16 + k, :], [sqr], rsd[:])
            nm = None
            for k in range(16):
                nm = P.add("dve", lambda e, k=k: e.scalar_tensor_tensor(out=B3[:, k, :], in0=B1[:, k, :], scalar=gn[:, k:k + 1], in1=rsd[:], op0=ALU.mult, op1=ALU.mult), [r0, prev["lastE"]])
            mg = None
            for m in range(16):
                tp = tmp[m % 2]
                b0, k0, m0 = group([(WGb[m, :, :], 16, "G")], lambda kg: B3[:, kg, :], [nm])
                b1, k1, m1 = group([(WGb[16 + m, :, :], 16, "G")], lambda kg: B3[:, kg, :], [])
                b2, k2, m2 = group([(WOb[m, :, :], 8, "O")], lambda kg: B5[:, kg, :], [lda])
                b3, k3, m3 = group([(WOb[16 + m, :, :], 8, "O")], lambda kg: B5[:, 8 + kg, :], [])
                a0 = P.add("act", lambda e, k0=k0, tp=tp, m=m: e.activation(out=tp[:, 0, :], in_=k0, func=AF.Sigmoid, bias=gn[:, 64 + m:65 + m]), [m0, mg])
                a1 = P.add("act", lambda e, k1=k1, tp=tp, m=m: e.activation(out=tp[:, 1, :], in_=k1, func=AF.Sigmoid, bias=gn[:, 80 + m:81 + m]), [m1])
                d0 = P.add("dve", lambda e, k2=k2, tp=tp: e.tensor_tensor(out=tp[:, 2, :], in0=k2, in1=tp[:, 0, :], op=ALU.mult), [m2, a0])
                d1 = P.add("dve", lambda e, k3=k3, tp=tp: e.tensor_tensor(out=tp[:, 3, :], in0=k3, in1=tp[:, 1, :], op=ALU.mult), [m3, a1])
                mg_new = P.add("dve", lambda e, tp=tp, m=m: e.tensor_tensor(out=B4[:, m, :], in0=tp[:, 2, :], in1=tp[:, 3, :], op=ALU.add), [prev["lastE"]])
                bank_free[b0] = a0; bank_free[b1] = a1; bank_free[b2] = d0; bank_free[b3] = d1
                mg_prev = mg; mg = mg_new
            ldx2 = P.dma("sp", lambda e, xsrc=xsrc: e.dma_start(out=B1[:], in_=xsrc), "x", [nm])
            sqs = []
            for m in range(16):
                b, bk, mm = group([(WOUTb[m, :, :], 16, "T")], lambda kg: B4[:, kg, :], [mg])
                c0 = P.add("act", lambda e, bk=bk, m=m: e.activation(out=B2[:, m, :], in_=bk, func=AF.Copy), [mm, prev["store"]])
                c1 = P.add("act", lambda e, bk=bk, m=m: e.activation(out=B4[:, 16 + m, :], in_=bk, func=AF.Square), [smm])
                bank_free[b] = c1; sqs.append(c1)
                prev["wout_last"] = mm
            smm, r1 = stats(lambda k: B4[:, 16 + k, :], [sqs[-1]], rsd[:])
            hl = None
            for m in range(16):
                P.add("dve", lambda e, m=m: e.scalar_tensor_tensor(out=B2[:, m, :], in0=B2[:, m, :], scalar=gn[:, 16 + m:17 + m], in1=rsd[:], op0=ALU.mult, op1=ALU.mult), [r1])
                hl = P.add("dve", lambda e, m=m: e.tensor_tensor(out=B2[:, m, :], in0=B2[:, m, :], in1=B1[:, m, :], op=ALU.add), [ldx2])
            sqr = P.add("act", lambda e: e.activation(out=B4[:, 16:32, :], in_=B2[:], func=AF.Square), [hl, smm])
            smm, r2 = stats(lambda k: B4[:, 16 + k, :], [sqr], rsd[:])
            hn = None
            for m in range(16):
                hn = P.add("dve", lambda e, m=m: e.scalar_tensor_tensor(out=B3[:, m, :], in0=B2[:, m, :], scalar=gn[:, 32 + m:33 + m], in1=rsd[:], op0=ALU.mult, op1=ALU.mult), [r2, prev["wout_last"]])
            al = None
            for f in range(nffn):
                tp = tmp[f % 2]
                b0, k0, m0 = group([(WGUb[f, :, :], 16, "U")], lambda kg: B3[:, kg, :], [hn])
                b1, k1, m1 = group([(WGUb[44 + f, :, :], 16, "U")], lambda kg: B3[:, kg, :], [])
                a0 = P.add("act", lambda e, k0=k0, tp=tp: e.activation(out=tp[:, 0, :], in_=k0, func=AF.Silu), [m0, al])
                al = P.add("dve", lambda e, k1=k1, tp=tp, f=f: e.tensor_tensor(out=B4[:, f, :], in0=k1, in1=tp[:, 0, :], op=ALU.mult), [m1, a0, smm, prev["wout_last"]])
                bank_free[b0] = a0; bank_free[b1] = al
                prev["gu_last"] = m1
            sqs = []
            for m in range(16):
                parts = [(WDb[m, :, 0:2048], 16, "D"), (WDb[m, :, 2048:4096], 16, "D"), (WDb[m, :, 4096:5632], 12, "D")]
                b, bk, mm = group(parts, lambda kg: B4[:, kg, :], [al])
                c0 = P.add("act", lambda e, bk=bk, m=m: e.activation(out=B1[:, m, :], in_=bk, func=AF.Copy), [mm, hl])
                c1 = P.add("act", lambda e, bk=bk, m=m: e.activation(out=B3[:, m, :], in_=bk, func=AF.Square), [prev["gu_last"]])
                bank_free[b] = c1; sqs.append(c1)
                prev["lastE"] = mm
            smm, r3 = stats(lambda k: B3[:, k, :], [sqs[-1]], rsd[:])
            ol = None
            for m in range(16):
                P.add("dve", lambda e, m=m: e.scalar_tensor_tensor(out=B1[:, m, :], in0=B1[:, m, :], scalar=gn[:, 48 + m:49 + m], in1=rsd[:], op0=ALU.mult, op1=ALU.mult), [r3])
                ol = P.add("dve", lambda e, m=m: e.tensor_tensor(out=B1[:, m, :], in0=B1[:, m, :], in1=B2[:, m, :], op=ALU.add))
            so = P.dma("sp", lambda e, tok=tok: e.dma_start(out=outT[:, tok].rearrange("(k p) t -> p k t", p=128), in_=B1[:]), "o", [ol])
            prev["store"] = so; all_stores.append(so)
            prev["lastE"] = smm
        P.add("sp", lambda e: e.wait_ge(P.dsem["o"], P.dcount["o"]), all_stores[-1:])
        P.flush()
        print("L2 instrs", P.n_instr, flush=True)
    return nc
def l1_inputs(inputs, c, S):
    b, hp = c // 4, c % 4
    x = inputs["x"]; w_in = inputs["w_in"][0]
    xT = np.ascontiguousarray(x[b, :S].T)
    pos = np.ascontiguousarray(np.broadcast_to(inputs["positions"][b, :S][None, :], (128, S))).astype(np.int32)
    hs = [2 * hp, 2 * hp + 1]
    cols = []
    for base in (0, 1024, 2048):
        for h in hs:
            cols.append(w_in[:, base + h * 128: base + (h + 1) * 128])
    cols.append(w_in[:, 3072:3072 + 768]); cols.append(w_in[:, 3840:3840 + 512])
    kr = w_in[:, 4352:4416]
    rot = np.concatenate([kr[:, 32:64], kr[:, 0:32]], 1)
    cols += [kr, kr, rot, rot]
    w1 = np.ascontiguousarray(np.concatenate(cols, 1))
    assert w1.shape == (2048, 2304)
    w_uq = inputs["w_uq"][0]; w_ukv = inputs["w_ukv"][0]
    qc = [w_uq[:, h * 192: h * 192 + 128] for h in hs]
    rp = [w_uq[:, h * 192 + 128: h * 192 + 192] for h in hs]
    qc += rp + [np.concatenate([r[:, 32:64], r[:, 0:32]], 1) for r in rp]
    wuq = np.ascontiguousarray(np.concatenate(qc, 1)); assert wuq.shape == (768, 512)
    kc = [w_ukv[:, h * 256: h * 256 + 128] for h in hs] + [w_ukv[:, h * 256 + 128: h * 256 + 256] for h in hs]
    wukv = np.ascontiguousarray(np.concatenate(kc, 1))
    gains = np.concatenate([inputs["norm_pre_mix"][0].reshape(16, 128).T, inputs["q_norm"][0].reshape(6, 128).T, inputs["kv_norm"][0].reshape(4, 128).T], 1)
    return {"xT": xT, "pos": pos, "w1": w1, "wuq": wuq, "wukv": wukv, "gains": np.ascontiguousarray(gains, dtype=np.float32), **consts()}

def consts():
    cm = np.zeros((128, 6, 128), np.float32)
    cm[:, 0] = np.eye(128)
    j = np.arange(128)[:, None]; k = np.arange(128)[None, :]
    cm[:, 1] = (j >= k)
    cm[:, 2] = 1.0 / 2048; cm[:, 3] = 1.0 / 768; cm[:, 4] = 1.0 / 512; cm[:, 5] = 1.0
    mk = np.zeros((128, 8, 512), np.float32)
    p = np.arange(128)[:, None]; f = np.arange(512)[None, :]
    for jj in range(4):
        mk[:, jj] = (128 * jj + p < f)
        mk[:, 4 + jj] = (2 * jj + p // 64 <= f // 64)
    half = 32
    inv_freq = (np.float32(10000.0) ** (-np.arange(half, dtype=np.float32) / np.float32(half))).astype(np.float32)
    invf = np.tile(inv_freq, 4).reshape(128, 1).astype(np.float32)
    return {"cmat": cm, "masks": mk, "invf": invf}
def tile_w(W):
    K, N = W.shape
    return np.ascontiguousarray(W.reshape(K // 128, 128, N // 128, 128).transpose(2, 1, 0, 3).reshape(N // 128, 128, K))
def l2_weights(inputs):
    w_in = inputs["w_in"][0]
    WG = tile_w(w_in[:, 4416:4416 + 4096])
    WO = np.concatenate([tile_w(inputs["w_o_sb"][0]), tile_w(inputs["w_o_mla"][0])], 0)
    WOUT = tile_w(inputs["w_out"][0]); WGU = tile_w(inputs["w_gate_up"][0]); WD = tile_w(inputs["w_down"][0])
    g = lambda v: v.reshape(-1, 128).T
    gains = np.concatenate([g(inputs["norm_pre_mix"][0]), g(inputs["norm_post_mix"][0]), g(inputs["norm_pre_ffn"][0]), g(inputs["norm_post_ffn"][0]), g(inputs["b_gate"][0])], 1)
    return {"WG": WG, "WO": WO, "WOUT": WOUT, "WGU": WGU, "WD": WD, "gains2": np.ascontiguousarray(gains, dtype=np.float32),
            "cmat2": np.full((128, 128), 1.0 / 2048, np.float32)}

from concourse.bass_utils import run_bass_kernel_spmd

S_FULL = 16384

def kernel(**inputs):
    inputs = {k: np.asarray(v) for k, v in inputs.items()}
    S = S_FULL
    nc1 = build_l1(S)
    ims = [l1_inputs(inputs, c, S) for c in range(8)]
    res1 = run_bass_kernel_spmd(nc1, ims, core_ids=list(range(8)))
    OT = [np.asarray(r["OT"]) for r in res1.results]
    del ims
    W = l2_weights(inputs)
    NQ = S // 4
    ims2 = []
    for c in range(8):
        b, j = c // 4, c % 4
        tok = slice(j * NQ, (j + 1) * NQ)
        chunks = [OT[b * 4 + h // 2][h % 2][:, tok] for h in range(8)] + [OT[b * 4 + h // 2][2 + h % 2][:, tok] for h in range(8)]
        AT = np.ascontiguousarray(np.stack(chunks, 0))
        xT = np.ascontiguousarray(inputs["x"][b, tok].T)
        ims2.append({"xT": xT, "AT": AT, **W})
    nc2 = build_l2(NQ)
    res2 = run_bass_kernel_spmd(nc2, ims2, core_ids=list(range(8)))
    out = np.empty((2, S, 2048), np.float32)
    for c in range(8):
        b, j = c // 4, c % 4
        out[b, j * NQ:(j + 1) * NQ, :] = np.asarray(res2.results[c]["outT"]).T
    return out
```

```python
import concourse.bass as bass
import concourse.mybir as mybir

class Op:
    __slots__ = ("eng", "fn", "waits", "need", "sig", "dma_sem", "dma_val", "idx")
    def __init__(self, eng, fn, waits):
        self.eng = eng; self.fn = fn; self.waits = [w for w in waits if w is not None]
        self.need = False; self.sig = None; self.dma_sem = None; self.dma_val = None

class Prog:
    ENGS = ("pe", "act", "dve", "pool", "sp")
    def __init__(self, nc, stack):
        self.nc = nc
        self.stack = stack
        self.ops = {e: [] for e in self.ENGS}
        self.esem = {e: stack.enter_context(nc.semaphore("s_" + e)) for e in self.ENGS}
        self.ecount = {e: 0 for e in self.ENGS}
        self.dsem = {}
        self.dcount = {}
        self.seen = {e: {} for e in self.ENGS}
        self.n_instr = 0

    def add(self, eng, fn, waits=()):
        op = Op(eng, fn, waits)
        for w in op.waits:
            if w.dma_sem is None:
                w.need = True
        self.ops[eng].append(op)
        return op

    def dma(self, eng, fn, key, waits=()):
        op = Op(eng, fn, waits)
        for w in op.waits:
            if w.dma_sem is None:
                w.need = True
        if key not in self.dsem:
            self.dsem[key] = self.stack.enter_context(self.nc.semaphore("d_" + key))
            self.dcount[key] = 0
        self.dcount[key] += 16
        op.dma_sem = key; op.dma_val = self.dcount[key]
        self.ops[eng].append(op)
        return op

    def flush(self):
        nc = self.nc
        for e in self.ENGS:
            for op in self.ops[e]:
                if op.dma_sem is None and op.need:
                    self.ecount[e] += 1
                    op.sig = self.ecount[e]
        def replay(ename):
            def run(eng):
                seen = self.seen[ename]
                for op in self.ops[ename]:
                    for w in op.waits:
                        if w.dma_sem is not None:
                            k = "d_" + w.dma_sem; sem = self.dsem[w.dma_sem]; val = w.dma_val
                        else:
                            if w.eng == ename and False:
                                continue
                            k = "e_" + w.eng; sem = self.esem[w.eng]; val = w.sig
                            assert val is not None
                        if seen.get(k, 0) >= val:
                            continue
                        seen[k] = val
                        eng.wait_ge(sem, val)
                        self.n_instr += 1
                    ins = op.fn(eng)
                    self.n_instr += 1
                    if op.dma_sem is not None:
                        ins.then_inc(self.dsem[op.dma_sem], 16)
                    elif op.need:
                        ins.then_inc(self.esem[ename], 1)
            return run
        with nc.Block() as block:
            block.tensor(replay("pe"))
            block.scalar(replay("act"))
            block.vector(replay("dve"))
            block.gpsimd(replay("pool"))
            block.sync(replay("sp"))
        self.ops = {e: [] for e in self.ENGS}

import contextlib, math
import numpy as np
import concourse.bass as bass
import concourse.mybir as mybir
F32 = mybir.dt.float32; BF16 = mybir.dt.bfloat16; I32 = mybir.dt.int32
AF = mybir.ActivationFunctionType; ALU = mybir.AluOpType
EPS = 1e-6
TWO_PI = 2.0 * math.pi
C1 = 6.28125; C2 = float(np.float32(TWO_PI - 6.28125)); C3 = float(TWO_PI - 6.28125 - np.float64(np.float32(TWO_PI - 6.28125)))

def build_l1(S, phaseB=True, dbg=False):
    TA = 256; NTA = S // TA; NB = S // 128; NQT = S // 512
    nc = bass.Bass("TRN2", target_bir_lowering=False)
    D = lambda name, shape, dt, kind="ExternalInput": nc.dram_tensor(name, shape, dt, kind=kind).ap()
    xT = D("xT", [2048, S], F32)
    pos = D("pos", [128, S], I32)
    w1 = D("w1", [2048, 2304], F32)
    wuq = D("wuq", [768, 512], F32)
    wukv = D("wukv", [512, 512], F32)
    gains = D("gains", [128, 26], F32)
    cmat = D("cmat", [128, 7, 128], F32)
    masks = D("masks", [128, 8, 512], F32)
    invf = D("invf", [128, 1], F32)
    FT = D("FT", [10, 128, S], BF16, "ExternalOutput" if dbg else "Internal")
    Vs = D("Vs", [4, 128, NB, 128], BF16, "ExternalOutput" if dbg else "Internal")
    OT = D("OT", [4, 128, S], BF16, "ExternalOutput")
    with contextlib.ExitStack() as st0:
        P = Prog(nc, st0)
        T0 = lambda name, shape, dt: st0.enter_context(nc.sbuf_tensor(name, shape, dt))
        cm = T0("cm", [128, 7, 128], BF16)
        gn = T0("gn", [128, 26], F32)
        ivf = T0("ivf", [128, 1], F32)
        cst = T0("cst", [128, 4], F32)
        ld_c = [P.dma("pool", lambda e: e.dma_start(out=cm[:], in_=cmat[:, :, :]), "c"),
                P.dma("sp", lambda e: e.dma_start(out=gn[:], in_=gains[:, :]), "c"),
                P.dma("sp", lambda e: e.dma_start(out=ivf[:], in_=invf[:, :]), "c")]
        ms = [P.add("pool", lambda e: e.memset(cst[:, 0:1], EPS)),
              P.add("pool", lambda e: e.memset(cst[:, 1:2], 1.0))]
        CONST = ld_c + ms
        ident = cm[:, 0, :]; tri = cm[:, 1, :]; c2048 = cm[:, 2, :]; c768 = cm[:, 3, :]; c512 = cm[:, 4, :]; ones = cm[:, 5, :]; ctri = cm[:, 6, :]
        with contextlib.ExitStack() as st:
            T = lambda name, shape, dt: st.enter_context(nc.sbuf_tensor(name, shape, dt))
            stA = [T(f"stA{i}", [128, 6, TA], BF16) for i in range(2)]
            W1 = T("W1", [128, 16, 2304], BF16)
            WQ = T("WQ", [128, 6, 512], BF16)
            WKV = T("WKV", [128, 4, 512], BF16)
            xf = T("xf", [128, 16, TA], F32)
            sq = T("sq", [128, 16, TA], BF16)
            xb = [T(f"xb{i}", [128, 16, TA], BF16) for i in range(2)]
            stB = [T(f"stB{i}", [128, 8, TA], BF16) for i in range(2)]
            ql = T("ql", [128, 6, TA], BF16); sqq = T("sqq", [128, 6, TA], BF16); qn2 = [T(f"qn{i}", [128, 6, TA], BF16) for i in range(2)]
            kvl = T("kvl", [128, 4, TA], BF16); sqkv = T("sqkv", [128, 4, TA], BF16); kvn2 = [T(f"kvn{i}", [128, 4, TA], BF16) for i in range(2)]
            rs = [T(f"rs{i}", [128, 3, TA], F32) for i in range(2)]
            lnt = T("lnt", [128, 3, TA], F32)
            pi = [T(f"pi{i}", [128, TA], I32) for i in range(2)]
            rp = T("rp", [128, 8, TA], F32)
            ki = T("ki", [128, TA], I32)
            cs = [T(f"cs{i}", [128, 2, TA], F32) for i in range(3)]
            vtok = [T(f"vtok{i}", [128, 8, 128], BF16) for i in range(2)]
            psf = st.enter_context(nc.psum_tensor("psf", [128, 7, 512], F32))
            pst = st.enter_context(nc.psum_tensor("pst", [128, 1024], BF16))
            wl = []
            for k in range(16):
                for c0 in range(0, 2304, 1152):
                    wl.append(P.dma("pool", lambda e, k=k, c0=c0: e.dma_start(out=W1[:, k, c0:c0 + 1152], in_=w1[k * 128:(k + 1) * 128, c0:c0 + 1152]), "w"))
            for k in range(6):
                wl.append(P.dma("pool", lambda e, k=k: e.dma_start(out=WQ[:, k, :], in_=wuq[k * 128:(k + 1) * 128, :]), "w"))
            for k in range(4):
                wl.append(P.dma("pool", lambda e, k=k: e.dma_start(out=WKV[:, k, :], in_=wukv[k * 128:(k + 1) * 128, :]), "w"))
            ng = []
            for base in (2176, 2240):
                ng.append(P.add("dve", lambda e, b=base: e.tensor_scalar(out=W1[:, :, b:b + 32], in0=W1[:, :, b:b + 32], scalar1=-1.0, scalar2=None, op0=ALU.mult), wl))
            for base in (384, 448):
                ng.append(P.add("dve", lambda e, b=base: e.tensor_scalar(out=WQ[:, :, b:b + 32], in0=WQ[:, :, b:b + 32], scalar1=-1.0, scalar2=None, op0=ALU.mult), wl))
            WREADY = wl + ng + CONST
            NS1 = 3; NS2 = 2
            def slot1(i): i %= NS1; return psf[:, 2 + i, 0:256]
            def slot2(i): i %= NS2; return psf[:, 5 + i, 0:256]
            s1_free = [None] * NS1; s2_free = [None] * NS2
            s1_cnt = [0]; s2_cnt = [0]
            stat_free = [None, None]
            stat_cnt = [0]
            def statslot():
                i = stat_cnt[0] % 2; stat_cnt[0] += 1
                return i, psf[:, i, 0:256]
            st_ = {}
            last = {"norm": [], "sqr": None, "ssx": None, "s1": {}, "s2": {}, "stA_store": {}, "stB_store": {}, "vstore": {}, "tr_evac": None,
                    "sqq_read": None, "sqkv_read": None, "qn_read": None, "kvn_read": None, "ql_w": None, "cs_read": {}, "rs_read": {}}
            stores = []

            def rstd_chain(ps_ap, slot_i, out_ap, idx, waits):
                a = P.add("act", lambda e: e.activation(out=lnt[:, idx, :], in_=ps_ap, func=AF.Ln, bias=cst[:, 0:1]), waits)
                b = P.add("act", lambda e: e.activation(out=out_ap, in_=lnt[:, idx, :], func=AF.Exp, scale=-0.5))
                stat_free[slot_i] = a
                return b

            def front(t):
                d = st_.setdefault(t, {})
                tok = slice(t * TA, (t + 1) * TA)
                ld = P.dma("sp", lambda e: e.dma_start(out=xf[:], in_=xT[:, tok].rearrange("(k p) t -> p k t", p=128)), "x", last["norm"] + [last["sqr"]])
                ldp = P.dma("sp", lambda e: e.dma_start(out=pi[t % 2][:], in_=pos[:, tok]), "p%d" % (t % 2), [st_.get(t - 2, {}).get("posf")])
                sqr = P.add("act", lambda e: e.activation(out=sq[:], in_=xf[:], func=AF.Square), [ld, last["ssx"]])
                last["sqr"] = sqr
                si, sap = statslot()
                mm = None
                for k in range(16):
                    mm = P.add("pe", lambda e, k=k: e.matmul(sap, lhsT=c2048, rhs=sq[:, k, :], start=(k == 0), stop=(k == 15)),
                               [sqr, stat_free[si]] + (CONST if t == 0 else []))
                last["ssx"] = mm
                r = rs[t % 2]
                rr = rstd_chain(sap, si, r[:, 0, :], 0, [mm] + last["rs_read"].get(t % 2, []))
                nm = []
                prev_s1 = last["s1"].get(t - 2)
                for k in range(16):
                    eng = "dve"
                    nm.append(P.add(eng, lambda e, k=k: e.scalar_tensor_tensor(out=xb[t % 2][:, k, :], in0=xf[:, k, :], scalar=gn[:, k:k + 1], in1=r[:, 0, :],
                                                                               op0=ALU.mult, op1=ALU.mult), [rr, prev_s1, ld]))
                last["norm"] = nm[-2:]
                d["norm"] = nm[-2:]
                c = cs[t % 3]
                w0 = last["cs_read"].get(t % 3, [])
                o = P.add("dve", lambda e: e.tensor_copy(out=rp[:, 0, :], in_=pi[t % 2][:]), [ldp])
                d["posf"] = o
                P.add("dve", lambda e: e.tensor_scalar(out=rp[:, 1, :], in0=rp[:, 0, :], scalar1=ivf[:, 0:1], scalar2=None, op0=ALU.mult))
                P.add("dve", lambda e: e.tensor_scalar(out=ki[:], in0=rp[:, 1, :], scalar1=1.0 / TWO_PI, scalar2=None, op0=ALU.mult))
                P.add("dve", lambda e: e.tensor_copy(out=rp[:, 2, :], in_=ki[:]))
                P.add("dve", lambda e: e.scalar_tensor_tensor(out=rp[:, 3, :], in0=rp[:, 2, :], scalar=-C1, in1=rp[:, 1, :], op0=ALU.mult, op1=ALU.add), [last.get("sh")])
                P.add("dve", lambda e: e.scalar_tensor_tensor(out=rp[:, 3, :], in0=rp[:, 2, :], scalar=-C2, in1=rp[:, 3, :], op0=ALU.mult, op1=ALU.add))
                P.add("dve", lambda e: e.scalar_tensor_tensor(out=rp[:, 3, :], in0=rp[:, 2, :], scalar=-C3, in1=rp[:, 3, :], op0=ALU.mult, op1=ALU.add))
                rcl = P.add("dve", lambda e: e.tensor_scalar(out=rp[:, 3, :], in0=rp[:, 3, :], scalar1=math.pi, scalar2=-math.pi, op0=ALU.min, op1=ALU.max))
                sn = P.add("act", lambda e: e.activation(out=c[:, 1, :], in_=rp[:, 3, :], func=AF.Sin), [rcl] + w0)
                sh = P.add("act", lambda e: e.activation(out=rp[:, 4, :], in_=rp[:, 3, :], func=AF.Sin, scale=0.5), [last.get("shsq")])
                last["sh"] = sh
                last["shsq"] = P.add("dve", lambda e: e.tensor_tensor(out=rp[:, 5, :], in0=rp[:, 4, :], in1=rp[:, 4, :], op=ALU.mult), [sh])
                csr = P.add("dve", lambda e: e.tensor_scalar(out=c[:, 0, :], in0=rp[:, 5, :], scalar1=-2.0, scalar2=1.0, op0=ALU.mult, op1=ALU.add), w0)
                d["cs"] = [sn, csr]

            def mgroup(slot_ap, free_op, lhs_fn, rhs_fn, nk, waits):
                mm = None
                for k in range(nk):
                    mm = P.add("pe", lambda e, k=k: e.matmul(slot_ap, lhsT=lhs_fn(k), rhs=rhs_fn(k), start=(k == 0), stop=(k == nk - 1)),
                               (waits + [free_op]) if k == 0 else [])
                return mm

            def stage1(t):
                d = st_[t]
                qn = qn2[t % 2]; kvn = kvn2[t % 2]
                X = xb[t % 2]
                wts = d["norm"] + (WREADY if t == 0 else [])
                A = stA[t % 2]
                mm_last = None
                evs = []
                for j in range(6):
                    i = s1_cnt[0]; s1_cnt[0] += 1
                    ap = slot1(i)
                    import os
                    EXP = os.environ.get("EXP", "")
                    mm = mgroup(ap, s1_free[i % NS1], (lambda k: c2048) if EXP == "b" else (lambda k, j=j: W1[:, k, j * 128:(j + 1) * 128]), lambda k: X[:, k, :], 16, wts)
                    ev = P.add("act", (lambda e: e.activation(out=lnt[:, 0, 0:1], in_=cst[:, 1:2], func=AF.Copy)) if EXP == "a" else (lambda e, ap=ap, j=j: e.activation(out=A[:, j, :], in_=ap, func=AF.Copy)),
                               [mm, last["stA_store"].get(t % 2), last["tr_in"].get(t % 2) if "tr_in" in last else None])
                    s1_free[i % NS1] = ev; evs.append(ev); mm_last = mm
                d["A_ev"] = evs
                import os
                SUB = int(os.environ.get("SUB", "9"))
                if SUB <= 1:
                    last["s1"][t] = mm_last; return
                qe = []
                for j in range(6):
                    i = s1_cnt[0]; s1_cnt[0] += 1
                    ap = slot1(i)
                    mm = mgroup(ap, s1_free[i % NS1], lambda k, j=j: W1[:, k, 768 + j * 128:768 + (j + 1) * 128], lambda k: X[:, k, :], 16, wts)
                    e1 = P.add("dve", lambda e, ap=ap, j=j: e.tensor_copy(out=ql[:, j, :], in_=ap), [mm, last["qn_w"] if "qn_w" in last else None])
                    e2 = P.add("act", lambda e, ap=ap, j=j: e.activation(out=sqq[:, j, :], in_=ql[:, j, :], func=AF.Square), [e1, last["sqq_read"]])
                    s1_free[i % NS1] = e1
                    qe.append((e1, e2)); mm_last = mm
                    s1_free[i % NS1] = e1
                si, sap = statslot()
                mm = None
                for k in range(6):
                    mm = P.add("pe", lambda e, k=k, sap=sap: e.matmul(sap, lhsT=c768, rhs=sqq[:, k, :], start=(k == 0), stop=(k == 5)), [qe[k][1], stat_free[si]])
                last["sqq_read"] = mm
                r = rs[t % 2]
                rq = rstd_chain(sap, si, r[:, 1, :], 1, [mm])
                qnw = None
                for k in range(6):
                    qnw = P.add("dve", lambda e, k=k: e.scalar_tensor_tensor(out=qn[:, k, :], in0=ql[:, k, :], scalar=gn[:, 16 + k:17 + k], in1=r[:, 1, :], op0=ALU.mult, op1=ALU.mult),
                                [rq, qe[k][0], last["qn_read"]])
                last["qn_w"] = qnw; d["qn"] = qnw
                if SUB <= 2:
                    last["s1"][t] = mm_last; return
                ke = []
                for j in range(4):
                    i = s1_cnt[0]; s1_cnt[0] += 1
                    ap = slot1(i)
                    mm = mgroup(ap, s1_free[i % NS1], lambda k, j=j: W1[:, k, 1536 + j * 128:1536 + (j + 1) * 128], lambda k: X[:, k, :], 16, wts)
                    e1 = P.add("dve", lambda e, ap=ap, j=j: e.tensor_copy(out=kvl[:, j, :], in_=ap), [mm, last["kvn_w"] if "kvn_w" in last else None])
                    e2 = P.add("act", lambda e, ap=ap, j=j: e.activation(out=sqkv[:, j, :], in_=kvl[:, j, :], func=AF.Square), [e1, last["sqkv_read"]])
                    ke.append((e1, e2))
                    s1_free[i % NS1] = e1
                si, sap = statslot()
                for k in range(4):
                    mm = P.add("pe", lambda e, k=k, sap=sap: e.matmul(sap, lhsT=c512, rhs=sqkv[:, k, :], start=(k == 0), stop=(k == 3)), [ke[k][1], stat_free[si]])
                last["sqkv_read"] = mm
                rk = rstd_chain(sap, si, r[:, 2, :], 2, [mm])
                kw = None
                for k in range(4):
                    kw = P.add("dve", lambda e, k=k: e.scalar_tensor_tensor(out=kvn[:, k, :], in0=kvl[:, k, :], scalar=gn[:, 22 + k:23 + k], in1=r[:, 2, :], op0=ALU.mult, op1=ALU.mult),
                               [rk, ke[k][0], last["kvn_read"]])
                last["kvn_w"] = kw; d["kvn"] = kw
                last["rs_read"][t % 2] = [qnw, kw]
                if SUB <= 3:
                    last["s1"][t] = mm_last; return
                B = stB[t % 2]
                c = cs[t % 3]
                aps = []
                for j in range(2):
                    i = s1_cnt[0]; s1_cnt[0] += 1
                    ap = slot1(i)
                    mm = mgroup(ap, s1_free[i % NS1], lambda k, j=j: W1[:, k, 2048 + j * 128:2048 + (j + 1) * 128], lambda k: X[:, k, :], 16, wts)
                    aps.append((ap, mm, i)); mm_last = mm
                t1 = P.add("dve", lambda e: e.tensor_tensor(out=rp[:, 6, :], in0=aps[0][0], in1=c[:, 0, :], op=ALU.mult), [aps[0][1]] + d["cs"])
                t2 = P.add("dve", lambda e: e.tensor_tensor(out=rp[:, 7, :], in0=aps[1][0], in1=c[:, 1, :], op=ALU.mult), [aps[1][1]])
                kr = P.add("dve", lambda e: e.tensor_tensor(out=B[:, 5, :], in0=rp[:, 6, :], in1=rp[:, 7, :], op=ALU.add), [last["stB_store"].get(t % 2)])
                s1_free[aps[0][2] % NS1] = t1; s1_free[aps[1][2] % NS1] = t2
                d["kr"] = kr
                last["s1"][t] = mm_last
                if SUB <= 4: return
                tok = slice(t * TA, (t + 1) * TA)
                so = P.dma("sp", lambda e: e.dma_start(out=FT[0:4, :, tok].rearrange("f p t -> p f t"), in_=A[:, 0:4, :]), "sa%d" % (t % 2), evs[0:4])
                stores.append(so); d["stA_store"] = so

            def stage2(t):
                d = st_[t]
                qn = qn2[t % 2]; kvn = kvn2[t % 2]
                B = stB[t % 2]; A = stA[t % 2]; c = cs[t % 3]
                tok = slice(t * TA, (t + 1) * TA)
                evs = []
                def grp(lhs_fn, rhs_fn, nk, waits):
                    i = s2_cnt[0]; s2_cnt[0] += 1
                    ap = slot2(i)
                    mm = mgroup(ap, s2_free[i % NS2], lhs_fn, rhs_fn, nk, waits)
                    return ap, mm, i % NS2
                wq = [d["qn"]]; wk = [d["kvn"]]
                bst = [last["stB_store"].get(t % 2), last["tr_in"].get(t % 2) if "tr_in" in last else None]
                for h in range(2):
                    ap, mm, i = grp(lambda k, h=h: WQ[:, k, h * 128:(h + 1) * 128], lambda k: qn[:, k, :], 6, wq)
                    ev = P.add("act", lambda e, ap=ap, h=h: e.activation(out=B[:, h, :], in_=ap, func=AF.Copy), [mm] + bst)
                    s2_free[i] = ev; evs.append(ev)
                apA, mmA, iA = grp(lambda k: WQ[:, k, 256:384], lambda k: qn[:, k, :], 6, wq)
                apB, mmB, iB = grp(lambda k: WQ[:, k, 384:512], lambda k: qn[:, k, :], 6, wq)
                last["qn_read"] = mmB
                t1 = P.add("dve", lambda e: e.tensor_tensor(out=rp[:, 6, :], in0=apA, in1=c[:, 0, :], op=ALU.mult), [mmA])
                t2 = P.add("dve", lambda e: e.tensor_tensor(out=rp[:, 7, :], in0=apB, in1=c[:, 1, :], op=ALU.mult), [mmB])
                qr = P.add("dve", lambda e: e.tensor_tensor(out=B[:, 2, :], in0=rp[:, 6, :], in1=rp[:, 7, :], op=ALU.add), bst)
                s2_free[iA] = t1; s2_free[iB] = t2
                last["cs_read"][t % 3] = [t1, t2]
                evs.append(qr)
                for h in range(4):
                    ap, mm, i = grp(lambda k, h=h: WKV[:, k, h * 128:(h + 1) * 128], lambda k: kvn[:, k, :], 4, wk)
                    dst = (3 + h) if h < 2 else (6 + h - 2)
                    ev = P.add("act", lambda e, ap=ap, dst=dst: e.activation(out=B[:, dst, :], in_=ap, func=AF.Copy), [mm] + bst)
                    s2_free[i] = ev; evs.append(ev)
                    if h == 3: last["kvn_read"] = mm
                evs.append(d["kr"])
                so = P.dma("sp", lambda e: e.dma_start(out=FT[4:10, :, tok].rearrange("f p t -> p f t"), in_=B[:, 0:6, :]), "sb%d" % (t % 2), evs)
                stores.append(so); last["stB_store"][t % 2] = so
                srcs = [A[:, 4, :], A[:, 5, :], B[:, 6, :], B[:, 7, :]]
                tr = None
                for h4 in range(4):
                    for j in range(2):
                        tr = P.add("pe", lambda e, h4=h4, j=j: e.transpose(pst[:, (h4 * 2 + j) * 128:(h4 * 2 + j + 1) * 128], srcs[h4][:, j * 128:(j + 1) * 128], ident),
                                   d["A_ev"][4:6] + evs[5:7] + [last["tr_evac"]])
                last.setdefault("tr_in", {})[t % 2] = tr
                vt = vtok[t % 2]
                ev = P.add("act", lambda e: e.activation(out=vt[:].rearrange("p a d -> p (a d)"), in_=pst[:, :], func=AF.Copy), [tr, last["vstore"].get(t % 2)])
                last["tr_evac"] = ev
                so2 = P.dma("sp", lambda e: e.dma_start(out=Vs[:, :, 2 * t:2 * t + 2, :].rearrange("h p j d -> p h j d"), in_=vt[:].rearrange("p (h j) d -> p h j d", j=2)), "sv%d" % (t % 2), [ev])
                stores.append(so2); last["vstore"][t % 2] = so2
                last["stA_store"][t % 2] = d["stA_store"]

            import os
            CUT = int(os.environ.get("CUT", "9"))
            if CUT >= 2: front(0)
            for t in range(NTA + 1):
                if CUT < 2: break
                if t + 1 < NTA: front(t + 1)
                if t < NTA and CUT >= 3: stage1(t)
                if t >= 1 and CUT >= 4: stage2(t - 1)
            if CUT < 9:
                P.add("dve", lambda e: e.tensor_copy(out=cst[:, 3:4], in_=cst[:, 1:2]), WREADY)
            barA = P.add("sp", lambda e: e.wait_ge(P.esem["sp"], 0), stores + last["norm"])
            barA.need = True
            P.flush()
        print("phase A instrs", P.n_instr, flush=True)
        if not phaseB:
            return nc
        with contextlib.ExitStack() as st:
            T = lambda name, shape, dt: st.enter_context(nc.sbuf_tensor(name, shape, dt))
            KT = T("KT", [128, S], BF16)
            KR = T("KR", [128, S], BF16)
            V = T("V", [128, NB, 128], BF16)
            qb = [T(f"qb{i}", [128, 2, 512], BF16) for i in range(2)]
            u = [T(f"u{i}", [128, 512], F32) for i in range(3)]
            sp_ = [T(f"sp{i}", [128, 512], BF16) for i in range(3)]
            ec = [T(f"ec{i}", [128, 512], F32) for i in range(2)]
            w = [T(f"wt{i}", [128, 512], BF16) for i in range(3)]
            Sb = [T(f"Sb{i}", [128, 512], BF16) for i in range(3)]
            Sf = [T(f"Sf{i}", [128, 512], F32) for i in range(3)]
            ost = [T(f"ost{i}", [128, 512], BF16) for i in range(2)]
            rec = T("rec", [128, 512], F32)
            mk = T("mk", [128, 8, 512], BF16)
            for jj in range(8):
                mkl = P.dma("pool", lambda e, jj=jj: e.dma_start(out=mk[:, jj, :], in_=masks[:, jj, :]), "c", [barA])
            ps = st.enter_context(nc.psum_tensor("psB", [128, 8, 512], F32))
            Sbank = [ps[:, 0, :], ps[:, 1, :]]; Cbank = [ps[:, 2, :], ps[:, 3, :]]; Obank = [ps[:, 4, :], ps[:, 5, :]]; Dbank = [ps[:, 6, :], ps[:, 7, :]]
            KCH = min(2048, S); NKC = S // KCH
            VCH = KCH // 128
            prev_head_S = [None]; prev_head_O = [None]
            out_stores = []
            ost_store = [None, None]
            obank_free = [None, None]
            qb_free = [None, None]
            scale_sb = 128 ** -0.5; scale_mla = 192 ** -0.5
            krl = None
            for hd in range(4):
                mla = hd >= 2; h = hd % 2
                fk = (2 + h) if not mla else (7 + h)
                fq = h if not mla else (4 + h)
                kl = [P.dma("sp", lambda e, c=c, fk=fk: e.dma_start(out=KT[:, c * KCH:(c + 1) * KCH], in_=FT[fk, :, c * KCH:(c + 1) * KCH]), "k%d" % c, [prev_head_S[0], barA]) for c in range(NKC)]
                vl = [P.dma("sp", lambda e, c=c, hd=hd: e.dma_start(out=V[:, c * VCH:(c + 1) * VCH, :], in_=Vs[hd, :, c * VCH:(c + 1) * VCH, :]), "v%d" % c, [prev_head_O[0], barA]) for c in range(NKC)]
                if hd == 2:
                    krl = [P.dma("sp", lambda e, c=c: e.dma_start(out=KR[:, c * KCH:(c + 1) * KCH], in_=FT[9, :, c * KCH:(c + 1) * KCH]), "kr%d" % c, [barA]) for c in range(NKC)]
                steps = []
                for qt in range(NQT):
                    nkb = 4 * qt + 4
                    for n, kb in enumerate(range(nkb - 1, -1, -1)):
                        steps.append(dict(qt=qt, kb=kb, first=(n == 0), last=(n == nkb - 1), j=(kb - 4 * qt) if kb >= 4 * qt else None))
                n = len(steps)
                R = {}
                qload = {}
                def get_q(qt):
                    if qt not in qload:
                        q = qb[qt % 2]
                        l = [P.dma("sp", lambda e, fq=fq: e.dma_start(out=q[:, 0, :], in_=FT[fq, :, qt * 512:(qt + 1) * 512]), "qa%d" % (qt % 2), [qb_free[qt % 2], barA])]
                        if mla:
                            l.append(P.dma("sp", lambda e: e.dma_start(out=q[:, 1, :], in_=FT[6, :, qt * 512:(qt + 1) * 512]), "qb%d" % (qt % 2), [qb_free[qt % 2], barA]))
                        qload[qt] = l
                    return qload[qt]
                for it in range(n + 3):
                    if it < n and steps[it]["first"]:
                        get_q(steps[it]["qt"])
                        if steps[it]["qt"] + 1 < NQT: get_q(steps[it]["qt"] + 1)
                    if it < n:
                        s = steps[it]; r = R.setdefault(it, {})
                        q = qb[s["qt"] % 2]; kb = s["kb"]
                        ksl = slice(kb * 128, (kb + 1) * 128)
                        wts = [kl[kb * 128 // KCH]] + get_q(s["qt"]) + [R.get(it - 2, {}).get("exp")]
                        if not mla:
                            r["S"] = P.add("pe", lambda e, q=q, ksl=ksl, it=it: e.matmul(Sbank[it % 2], lhsT=KT[:, ksl], rhs=q[:, 0, :], start=True, stop=True), wts)
                        else:
                            P.add("pe", lambda e, q=q, ksl=ksl, it=it: e.matmul(Sbank[it % 2], lhsT=KT[:, ksl], rhs=q[:, 0, :], start=True, stop=False), wts + [krl[kb * 128 // KCH]])
                            r["S"] = P.add("pe", lambda e, q=q, ksl=ksl, it=it, h=h: e.matmul(Sbank[it % 2], lhsT=KR[64 * h:64 * h + 64, ksl], rhs=q[64 * h:64 * h + 64, 1, :], start=False, stop=True))
                        if s["last"]:
                            qb_free[s["qt"] % 2] = r["S"]
                        prev_head_S[0] = r["S"]
                    if not mla:
                        if it < n:
                            s = steps[it]; r = R[it]
                            r["exp"] = P.add("act", lambda e, it=it: e.activation(out=u[it % 3][:], in_=Sbank[it % 2], func=AF.Exp, scale=scale_sb), [r["S"], R.get(it - 3, {}).get("w")])
                            r["ln"] = P.add("act", lambda e, it=it: e.activation(out=sp_[it % 3][:], in_=u[it % 3][:], func=AF.Ln, bias=cst[:, 1:2]),
                                            [R.get(it - 3, {}).get("lastread"), R.get(it - 3, {}).get("compl")])
                            spr = r["ln"]
                            if s["j"] is not None:
                                spr = P.add("dve", lambda e, it=it, j=s["j"]: e.tensor_tensor(out=sp_[it % 3][:], in0=sp_[it % 3][:], in1=mk[:, j, :], op=ALU.mult), [r["ln"], mkl])
                            r["sp"] = spr
                        i1 = it - 1; i2 = it - 2
                        if 0 <= i2 < n:
                            s2 = steps[i2]
                            if i2 + 2 < n and steps[i2 + 2]["qt"] == s2["qt"]:
                                R[i2]["compl"] = P.add("pe", lambda e, i2=i2: e.matmul(Cbank[i2 % 2], lhsT=ctri, rhs=sp_[i2 % 3][:], start=False, stop=True), [R[i2]["expC"]])
                        if 0 <= i1 < n:
                            s = steps[i1]; r = R[i1]
                            r["C"] = P.add("pe", lambda e, i1=i1, st_=s["first"]: e.matmul(Cbank[i1 % 2], lhsT=tri, rhs=sp_[i1 % 3][:], start=st_, stop=True),
                                           [r["sp"], R.get(i1 - 2, {}).get("expC")])
                            r["lastread"] = r["C"]
                            if i1 + 1 < n and steps[i1 + 1]["qt"] == s["qt"]:
                                r["lastread"] = P.add("pe", lambda e, i1=i1, st_=s["first"]: e.matmul(Cbank[(i1 + 1) % 2], lhsT=ones, rhs=sp_[i1 % 3][:], start=st_, stop=True),
                                                      [R.get(i1 - 1, {}).get("expC")])
                            r["expC"] = P.add("act", lambda e, i1=i1: e.activation(out=ec[i1 % 2][:], in_=Cbank[i1 % 2], func=AF.Exp, scale=-1.0), [r["C"], R.get(i1 - 2, {}).get("w")])
                            wop = P.add("dve", lambda e, i1=i1: e.tensor_tensor(out=w[i1 % 3][:], in0=u[i1 % 3][:], in1=ec[i1 % 2][:], op=ALU.mult), [r["expC"], R.get(i1 - 3, {}).get("O")])
                            if s["j"] is not None:
                                wop = P.add("dve", lambda e, i1=i1, j=s["j"]: e.tensor_tensor(out=w[i1 % 3][:], in0=w[i1 % 3][:], in1=mk[:, j, :], op=ALU.mult), [mkl])
                            r["w"] = wop
                    else:
                        if it < n:
                            s = steps[it]; r = R[it]
                            pe = P.add("act", lambda e, it=it: e.activation(out=w[it % 3][:], in_=Sbank[it % 2], func=AF.Exp, scale=scale_mla), [r["S"], R.get(it - 3, {}).get("O")])
                            r["exp"] = pe
                            if s["j"] is not None:
                                pe = P.add("dve", lambda e, it=it, j=s["j"]: e.tensor_tensor(out=w[it % 3][:], in0=w[it % 3][:], in1=mk[:, 4 + j, :], op=ALU.mult), [pe, mkl])
                            r["w"] = pe
                    io = it - 2 if not mla else it - 1
                    if 0 <= io < n:
                        s = steps[io]; r = R[io]
                        qt = s["qt"]; kb = s["kb"]
                        wts = [r["w"], vl[kb * 128 // KCH]] + ([obank_free[qt % 2]] if s["first"] else [])
                        r["O"] = P.add("pe", lambda e, io=io, qt=qt, kb=kb, s=s: e.matmul(Obank[qt % 2], lhsT=V[:, kb, :], rhs=w[io % 3][:], start=s["first"], stop=s["last"]), wts)
                        if mla:
                            r["O"] = P.add("pe", lambda e, io=io, qt=qt, s=s: e.matmul(Dbank[qt % 2], lhsT=ones, rhs=w[io % 3][:], start=s["first"], stop=s["last"]))
                        prev_head_O[0] = r["O"]
                        if s["last"]:
                            o = ost[qt % 2]
                            if not mla:
                                ev = P.add("dve", lambda e, qt=qt, o=o: e.tensor_copy(out=o[:], in_=Obank[qt % 2]), [r["O"], ost_store[qt % 2]])
                            else:
                                rc = P.add("dve", lambda e, qt=qt: e.reciprocal(out=rec[:], in_=Dbank[qt % 2]), [r["O"]])
                                ev = P.add("dve", lambda e, qt=qt, o=o: e.tensor_tensor(out=o[:], in0=Obank[qt % 2], in1=rec[:], op=ALU.mult), [ost_store[qt % 2]])
                            obank_free[qt % 2] = ev
                            so = P.dma("sp", lambda e, qt=qt, o=o, hd=hd: e.dma_start(out=OT[hd, :, qt * 512:(qt + 1) * 512], in_=o[:]), "o%d" % (qt % 2), [ev])
                            ost_store[qt % 2] = so; out_stores.append(so)
            P.add("sp", lambda e: e.wait_ge(P.dsem["o0"], P.dcount["o0"]), out_stores[-2:])
            if "o1" in P.dsem:
                P.add("sp", lambda e: e.wait_ge(P.dsem["o1"], P.dcount["o1"]), [])
            P.flush()
        print("total instrs", P.n_instr, flush=True)
    return nc

def build_l2(NTOK, nffn=44):
    T = 512; NT = NTOK // T
    nc = bass.Bass("TRN2", target_bir_lowering=False)
    D = lambda name, shape, dt, kind="ExternalInput": nc.dram_tensor(name, shape, dt, kind=kind).ap()
    xT = D("xT", [2048, NTOK], F32)
    AT = D("AT", [16, 128, NTOK], BF16)
    WG = D("WG", [32, 128, 2048], F32); WO = D("WO", [32, 128, 1024], F32); WOUT = D("WOUT", [16, 128, 2048], F32)
    WGU = D("WGU", [88, 128, 2048], F32); WD = D("WD", [16, 128, 5632], F32)
    gains = D("gains2", [128, 96], F32)
    cmat = D("cmat2", [128, 128], F32)
    outT = D("outT", [2048, NTOK], F32, "ExternalOutput")
    WGb = D("WGb", [32, 128, 2048], BF16, "Internal"); WOb = D("WOb", [32, 128, 1024], BF16, "Internal"); WOUTb = D("WOUTb", [16, 128, 2048], BF16, "Internal")
    WGUb = D("WGUb", [88, 128, 2048], BF16, "Internal"); WDb = D("WDb", [16, 128, 5632], BF16, "Internal")
    with contextlib.ExitStack() as st:
        P = Prog(nc, st)
        Tn = lambda name, shape, dt: st.enter_context(nc.sbuf_tensor(name, shape, dt))
        B1 = Tn("B1", [128, 16, T], F32); B2 = Tn("B2", [128, 16, T], F32)
        B3 = Tn("B3", [128, 16, T], BF16); B4 = Tn("B4", [128, 44, T], BF16); B5 = Tn("B5", [128, 16, T], BF16)
        NSL = 6
        slots = [Tn(f"ws{i}", [128, 2048], BF16) for i in range(NSL)]
        tmp = [Tn(f"tmp{i}", [128, 4, T], F32) for i in range(2)]
        gn = Tn("gn", [128, 96], F32); c2048 = Tn("c2048", [128, 128], BF16); cst = Tn("cst", [128, 2], F32)
        rsd = Tn("rsd", [128, T], F32); lnt = Tn("lnt", [128, T], F32)
        ps = st.enter_context(nc.psum_tensor("ps", [128, 8, 512], F32))
        CONST = [P.dma("sp", lambda e: e.dma_start(out=gn[:], in_=gains[:, :]), "c"),
                 P.dma("pool", lambda e: e.dma_start(out=c2048[:], in_=cmat[:, :]), "c"),
                 P.add("dve", lambda e: e.memset(cst[:, 0:1], EPS))]
        cv = {}
        def conv(key, dst, src):
            cv[key] = P.dma("pool", lambda e: e.dma_start(out=dst, in_=src), "cv" + key)
        for m in range(32): conv("G", WGb[m, :, :], WG[m, :, :])
        for m in range(32): conv("O", WOb[m, :, :], WO[m, :, :])
        for m in range(16): conv("T", WOUTb[m, :, :], WOUT[m, :, :])
        for m in range(88): conv("U", WGUb[m, :, :], WGU[m, :, :])
        for m in range(16):
            for c0, c1 in ((0, 2048), (2048, 4096), (4096, 5632)):
                conv("D", WDb[m, :, c0:c1], WD[m, :, c0:c1])
        NBK = 6
        bank_free = [None] * NBK; bank_cnt = [0]
        slot_free = [None] * NSL; slot_cnt = [0]
        stat_free = [None]
        S = {"first": True}
        def wload(src_ap, ncols, cvkey):
            i = slot_cnt[0] % NSL; slot_cnt[0] += 1
            sl = slots[i]
            op = P.dma("sp", lambda e: e.dma_start(out=sl[:, 0:ncols], in_=src_ap), "w%d" % i, [slot_free[i], cv[cvkey]])
            return i, sl, op
        def group(parts, rhs_fn, waits):
            b = bank_cnt[0] % NBK; bank_cnt[0] += 1
            bank = ps[:, 1 + b, :]
            kg = 0; mm = None; nk = sum(kc for _, kc, _ in parts); first = True
            for src, kc, cvkey in parts:
                i, sl, ld = wload(src, kc * 128, cvkey)
                for k in range(kc):
                    w_ = [ld] if k == 0 else []
                    if first:
                        w_ = w_ + waits + [bank_free[b]] + (CONST if S["first"] else [])
                        first = False; S["first"] = False
                    mm = P.add("pe", lambda e, k=k, kg=kg, sl=sl: e.matmul(bank, lhsT=sl[:, k * 128:(k + 1) * 128], rhs=rhs_fn(kg), start=(kg == 0), stop=(kg == nk - 1)), w_)
                    kg += 1
                slot_free[i] = mm
            return b, bank, mm
        def stats(sq_fn, waits, out_ap):
            mm = None
            for k in range(16):
                mm = P.add("pe", lambda e, k=k: e.matmul(ps[:, 0, :], lhsT=c2048[:], rhs=sq_fn(k), start=(k == 0), stop=(k == 15)), (waits + [stat_free[0]] + CONST) if k == 0 else [])
            a = P.add("act", lambda e: e.activation(out=lnt[:], in_=ps[:, 0, :], func=AF.Ln, bias=cst[:, 0:1]), [mm])
            stat_free[0] = a
            b = P.add("act", lambda e: e.activation(out=out_ap, in_=lnt[:], func=AF.Exp, scale=-0.5), waits)
            return mm, b
        prev = {"store": None, "lastE": None, "gu_last": None, "wout_last": None}
        all_stores = []
        for t in range(NT):
            tok = slice(t * T, (t + 1) * T)
            xsrc = xT[:, tok].rearrange("(k p) t -> p k t", p=128)
            ldx = P.dma("sp", lambda e, xsrc=xsrc: e.dma_start(out=B1[:], in_=xsrc), "x", [prev["store"]])
            lda = P.dma("sp", lambda e, tok=tok: e.dma_start(out=B5[:], in_=AT[:, :, tok].rearrange("c p t -> p c t")), "a", [prev["wout_last"]])
            sq = B4[:, 16:32, :]
            sqr = P.add("act", lambda e: e.activation(out=sq, in_=B1[:], func=AF.Square), [ldx, prev["lastE"]])
            smm, r0 = stats(lambda k: B4[:, 16 + k, :], [sqr], rsd[:])
            nm = None
            for k in range(16):
                nm = P.add("dve", lambda e, k=k: e.scalar_tensor_tensor(out=B3[:, k, :], in0=B1[:, k, :], scalar=gn[:, k:k + 1], in1=rsd[:], op0=ALU.mult, op1=ALU.mult), [r0, prev["lastE"]])
            mg = None
            for m in range(16):
                tp = tmp[m % 2]
                b0, k0, m0 = group([(WGb[m, :, :], 16, "G")], lambda kg: B3[:, kg, :], [nm])
                b1, k1, m1 = group([(WGb[16 + m, :, :], 16, "G")], lambda kg: B3[:, kg, :], [])
                b2, k2, m2 = group([(WOb[m, :, :], 8, "O")], lambda kg: B5[:, kg, :], [lda])
                b3, k3, m3 = group([(WOb[16 + m, :, :], 8, "O")], lambda kg: B5[:, 8 + kg, :], [])
                a0 = P.add("act", lambda e, k0=k0, tp=tp, m=m: e.activation(out=tp[:, 0, :], in_=k0, func=AF.Sigmoid, bias=gn[:, 64 + m:65 + m]), [m0, mg])
                a1 = P.add("act", lambda e, k1=k1, tp=tp, m=m: e.activation(out=tp[:, 1, :], in_=k1, func=AF.Sigmoid, bias=gn[:, 80 + m:81 + m]), [m1])
                d0 = P.add("dve", lambda e, k2=k2, tp=tp: e.tensor_tensor(out=tp[:, 2, :], in0=k2, in1=tp[:, 0, :], op=ALU.mult), [m2, a0])
                d1 = P.add("dve", lambda e, k3=k3, tp=tp: e.tensor_tensor(out=tp[:, 3, :], in0=k3, in1=tp[:, 1, :], op=ALU.mult), [m3, a1])
                mg_new = P.add("dve", lambda e, tp=tp, m=m: e.tensor_tensor(out=B4[:, m, :], in0=tp[:, 2, :], in1=tp[:, 3, :], op=ALU.add), [prev["lastE"]])
                bank_free[b0] = a0; bank_free[b1] = a1; bank_free[b2] = d0; bank_free[b3] = d1
                mg_prev = mg; mg = mg_new
            ldx2 = P.dma("sp", lambda e, xsrc=xsrc: e.dma_start(out=B1[:], in_=xsrc), "x", [nm])
            sqs = []
            for m in range(16):
                b, bk, mm = group([(WOUTb[m, :, :], 16, "T")], lambda kg: B4[:, kg, :], [mg])
                c0 = P.add("act", lambda e, bk=bk, m=m: e.activation(out=B2[:, m, :], in_=bk, func=AF.Copy), [mm, prev["store"]])
                c1 = P.add("act", lambda e, bk=bk, m=m: e.activation(out=B4[:, 16 + m, :], in_=bk, func=AF.Square), [smm])
                bank_free[b] = c1; sqs.append(c1)
                prev["wout_last"] = mm
            smm, r1 = stats(lambda k: B4[:, 16 + k, :], [sqs[-1]], rsd[:])
            hl = None
            for m in range(16):
                P.add("dve", lambda e, m=m: e.scalar_tensor_tensor(out=B2[:, m, :], in0=B2[:, m, :], scalar=gn[:, 16 + m:17 + m], in1=rsd[:], op0=ALU.mult, op1=ALU.mult), [r1])
                hl = P.add("dve", lambda e, m=m: e.tensor_tensor(out=B2[:, m, :], in0=B2[:, m, :], in1=B1[:, m, :], op=ALU.add), [ldx2])
            sqr = P.add("act", lambda e: e.activation(out=B4[:, 16:32, :], in_=B2[:], func=AF.Square), [hl, smm])
            smm, r2 = stats(lambda k: B4[:, 16 + k, :], [sqr], rsd[:])
            hn = None
            for m in range(16):
                hn = P.add("dve", lambda e, m=m: e.scalar_tensor_tensor(out=B3[:, m, :], in0=B2[:, m, :], scalar=gn[:, 32 + m:33 + m], in1=rsd[:], op0=ALU.mult, op1=ALU.mult), [r2, prev["wout_last"]])
            al = None
            for f in range(nffn):
                tp = tmp[f % 2]
                b0, k0, m0 = group([(WGUb[f, :, :], 16, "U")], lambda kg: B3[:, kg, :], [hn])
                b1, k1, m1 = group([(WGUb[44 + f, :, :], 16, "U")], lambda kg: B3[:, kg, :], [])
                a0 = P.add("act", lambda e, k0=k0, tp=tp: e.activation(out=tp[:, 0, :], in_=k0, func=AF.Silu), [m0, al])
                al = P.add("dve", lambda e, k1=k1, tp=tp, f=f: e.tensor_tensor(out=B4[:, f, :], in0=k1, in1=tp[:, 0, :], op=ALU.mult), [m1, a0, smm, prev["wout_last"]])
                bank_free[b0] = a0; bank_free[b1] = al
                prev["gu_last"] = m1
            sqs = []
            for m in range(16):
                parts = [(WDb[m, :, 0:2048], 16, "D"), (WDb[m, :, 2048:4096], 16, "D"), (WDb[m, :, 4096:5632], 12, "D")]
                b, bk, mm = group(parts, lambda kg: B4[:, kg, :], [al])
                c0 = P.add("act", lambda e, bk=bk, m=m: e.activation(out=B1[:, m, :], in_=bk, func=AF.Copy), [mm, hl])
                c1 = P.add("act", lambda e, bk=bk, m=m: e.activation(out=B3[:, m, :], in_=bk, func=AF.Square), [prev["gu_last"]])
                bank_free[b] = c1; sqs.append(c1)
                prev["lastE"] = mm
            smm, r3 = stats(lambda k: B3[:, k, :], [sqs[-1]], rsd[:])
            ol = None
            for m in range(16):
                P.add("dve", lambda e, m=m: e.scalar_tensor_tensor(out=B1[:, m, :], in0=B1[:, m, :], scalar=gn[:, 48 + m:49 + m], in1=rsd[:], op0=ALU.mult, op1=ALU.mult), [r3])
                ol = P.add("dve", lambda e, m=m: e.tensor_tensor(out=B1[:, m, :], in0=B1[:, m, :], in1=B2[:, m, :], op=ALU.add))
            so = P.dma("sp", lambda e, tok=tok: e.dma_start(out=outT[:, tok].rearrange("(k p) t -> p k t", p=128), in_=B1[:]), "o", [ol])
            prev["store"] = so; all_stores.append(so)
            prev["lastE"] = smm
        P.add("sp", lambda e: e.wait_ge(P.dsem["o"], P.dcount["o"]), all_stores[-1:])
        P.flush()
        print("L2 instrs", P.n_instr, flush=True)
    return nc
def l1_inputs(inputs, c, S):
    b, hp = c // 4, c % 4
    x = inputs["x"]; w_in = inputs["w_in"][0]
    xT = np.ascontiguousarray(x[b, :S].T)
    pos = np.ascontiguousarray(np.broadcast_to(inputs["positions"][b, :S][None, :], (128, S))).astype(np.int32)
    hs = [2 * hp, 2 * hp + 1]
    cols = []
    for base in (0, 1024, 2048):
        for h in hs:
            cols.append(w_in[:, base + h * 128: base + (h + 1) * 128])
    cols.append(w_in[:, 3072:3072 + 768]); cols.append(w_in[:, 3840:3840 + 512])
    kr = w_in[:, 4352:4416]
    rot = np.concatenate([kr[:, 32:64], kr[:, 0:32]], 1)
    cols += [kr, kr, rot, rot]
    w1 = np.ascontiguousarray(np.concatenate(cols, 1))
    assert w1.shape == (2048, 2304)
    w_uq = inputs["w_uq"][0]; w_ukv = inputs["w_ukv"][0]
    qc = [w_uq[:, h * 192: h * 192 + 128] for h in hs]
    rp = [w_uq[:, h * 192 + 128: h * 192 + 192] for h in hs]
    qc += rp + [np.concatenate([r[:, 32:64], r[:, 0:32]], 1) for r in rp]
    wuq = np.ascontiguousarray(np.concatenate(qc, 1)); assert wuq.shape == (768, 512)
    kc = [w_ukv[:, h * 256: h * 256 + 128] for h in hs] + [w_ukv[:, h * 256 + 128: h * 256 + 256] for h in hs]
    wukv = np.ascontiguousarray(np.concatenate(kc, 1))
    gains = np.concatenate([inputs["norm_pre_mix"][0].reshape(16, 128).T, inputs["q_norm"][0].reshape(6, 128).T, inputs["kv_norm"][0].reshape(4, 128).T], 1)
    return {"xT": xT, "pos": pos, "w1": w1, "wuq": wuq, "wukv": wukv, "gains": np.ascontiguousarray(gains, dtype=np.float32), **consts()}

def consts():
    cm = np.zeros((128, 7, 128), np.float32)
    cm[:, 0] = np.eye(128)
    j = np.arange(128)[:, None]; k = np.arange(128)[None, :]
    cm[:, 1] = (j >= k)
    cm[:, 2] = 1.0 / 2048; cm[:, 3] = 1.0 / 768; cm[:, 4] = 1.0 / 512; cm[:, 5] = 1.0
    cm[:, 6] = (j < k)
    mk = np.zeros((128, 8, 512), np.float32)
    p = np.arange(128)[:, None]; f = np.arange(512)[None, :]
    for jj in range(4):
        mk[:, jj] = (128 * jj + p < f)
        mk[:, 4 + jj] = (2 * jj + p // 64 <= f // 64)
    half = 32
    inv_freq = (np.float32(10000.0) ** (-np.arange(half, dtype=np.float32) / np.float32(half))).astype(np.float32)
    invf = np.tile(inv_freq, 4).reshape(128, 1).astype(np.float32)
    return {"cmat": cm, "masks": mk, "invf": invf}
def tile_w(W):
    K, N = W.shape
    return np.ascontiguousarray(W.reshape(K // 128, 128, N // 128, 128).transpose(2, 1, 0, 3).reshape(N // 128, 128, K))
def l2_weights(inputs):
    w_in = inputs["w_in"][0]
    WG = tile_w(w_in[:, 4416:4416 + 4096])
    WO = np.concatenate([tile_w(inputs["w_o_sb"][0]), tile_w(inputs["w_o_mla"][0])], 0)
    WOUT = tile_w(inputs["w_out"][0]); WGU = tile_w(inputs["w_gate_up"][0]); WD = tile_w(inputs["w_down"][0])
    g = lambda v: v.reshape(-1, 128).T
    gains = np.concatenate([g(inputs["norm_pre_mix"][0]), g(inputs["norm_post_mix"][0]), g(inputs["norm_pre_ffn"][0]), g(inputs["norm_post_ffn"][0]), g(inputs["b_gate"][0])], 1)
    return {"WG": WG, "WO": WO, "WOUT": WOUT, "WGU": WGU, "WD": WD, "gains2": np.ascontiguousarray(gains, dtype=np.float32),
            "cmat2": np.full((128, 128), 1.0 / 2048, np.float32)}

from concourse.bass_utils import run_bass_kernel_spmd

S_FULL = 16384

def kernel(**inputs):
    inputs = {k: np.asarray(v) for k, v in inputs.items()}
    S = S_FULL
    nc1 = build_l1(S)
    ims = [l1_inputs(inputs, c, S) for c in range(8)]
    res1 = run_bass_kernel_spmd(nc1, ims, core_ids=list(range(8)))
    OT = [np.asarray(r["OT"]) for r in res1.results]
    del ims
    W = l2_weights(inputs)
    NQ = S // 4
    ims2 = []
    for c in range(8):
        b, j = c // 4, c % 4
        tok = slice(j * NQ, (j + 1) * NQ)
        chunks = [OT[b * 4 + h // 2][h % 2][:, tok] for h in range(8)] + [OT[b * 4 + h // 2][2 + h % 2][:, tok] for h in range(8)]
        AT = np.ascontiguousarray(np.stack(chunks, 0))
        xT = np.ascontiguousarray(inputs["x"][b, tok].T)
        ims2.append({"xT": xT, "AT": AT, **W})
    nc2 = build_l2(NQ)
    res2 = run_bass_kernel_spmd(nc2, ims2, core_ids=list(range(8)))
    out = np.empty((2, S, 2048), np.float32)
    for c in range(8):
        b, j = c // 4, c % 4
        out[b, j * NQ:(j + 1) * NQ, :] = np.asarray(res2.results[c]["outT"]).T
    return out
```
